# Optimizing a Trainium2 kernel written in Bass

```python
import jax, jax.numpy as jnp
from jax import lax
import numpy as np

D_MODEL = 1024
BATCH = 2
SEQ = 8192
DEPTH = 4

N_MIXERS = 2
N_LRU_LAYERS = (DEPTH + 1) // 2
N_POOL_LAYERS = DEPTH // 2
D_FF = 11 * D_MODEL // 4
D_RNN = 5 * D_MODEL // 4
LRU_HEADS = 16
LRU_HEAD_DIM = D_RNN // LRU_HEADS
CONV_WIDTH = 4
LRU_C = 8.0
POOL_WINDOWS = (2, 4, 8, 16)
POOL_GROUPS = len(POOL_WINDOWS)
POOL_GROUP_DIM = D_MODEL // POOL_GROUPS
PLE_DIM = 256
RMS_EPS = 1e-6

kernel_name = "hybrid_rglru_pool_macaron_ple"


def rms_norm(x, g):
    xf = x.astype(jnp.float32)
    y = xf * lax.rsqrt(jnp.mean(xf * xf, axis=-1, keepdims=True) + RMS_EPS)
    return (y * g.astype(jnp.float32)).astype(x.dtype)


def swiglu(x, w_gate, w_up, w_down):
    return (jax.nn.silu(x @ w_gate) * (x @ w_up)) @ w_down


def _lin_combine(c1, c2):
    a1, b1 = c1
    a2, b2 = c2
    return a1 * a2, a2 * b1 + b2


def rglru_mixer(x, w_in, conv_w, conv_b, w_a, b_a, w_x, b_x, a_param, w_out):
    B, S, _ = x.shape
    z = x @ w_in
    gate_branch, xb = z[..., :D_RNN], z[..., D_RNN:]
    xp = jnp.pad(xb, ((0, 0), (CONV_WIDTH - 1, 0), (0, 0)))
    xc = conv_b + conv_w[0] * xp[:, 0:S]
    for k in range(1, CONV_WIDTH):
        xc = xc + conv_w[k] * xp[:, k:k + S]
    xh = xc.reshape(B, S, LRU_HEADS, LRU_HEAD_DIM)
    r = jax.nn.sigmoid(jnp.einsum('bshi,hij->bshj', xh, w_a).reshape(B, S, D_RNN) + b_a)
    ig = jax.nn.sigmoid(jnp.einsum('bshi,hij->bshj', xh, w_x).reshape(B, S, D_RNN) + b_x)
    log_a = -LRU_C * r.astype(jnp.float32) * jax.nn.softplus(-a_param.astype(jnp.float32))
    a = jnp.exp(log_a)
    mult = jnp.sqrt(-jnp.expm1(2.0 * log_a))
    bterm = mult * (ig * xc).astype(jnp.float32)
    _, h = lax.associative_scan(_lin_combine, (a, bterm), axis=1)
    y = h.astype(x.dtype) * jax.nn.gelu(gate_branch)
    return y @ w_out


def pool_mixer(x, w, b, scale):
    B, S, _ = x.shape
    xf = x.astype(jnp.float32)
    cs = jnp.cumsum(xf, axis=1)
    t = jnp.arange(S)
    outs = []
    for g, win in enumerate(POOL_WINDOWS):
        lo, hi = g * POOL_GROUP_DIM, (g + 1) * POOL_GROUP_DIM
        c = cs[..., lo:hi]
        prev = jnp.pad(c[:, :S - win], ((0, 0), (win, 0), (0, 0)))
        count = jnp.minimum(t + 1, win).astype(jnp.float32)[None, :, None]
        outs.append((c - prev) / count - xf[..., lo:hi])
    u = jnp.stack(outs, axis=2).astype(x.dtype)
    y = jnp.einsum('bsgc,gcd->bsgd', u, w).reshape(B, S, D_MODEL)
    return (y + b) * scale


def setup_inputs(seed: int = 0) -> dict:
    key = jax.random.key(seed)
    ks = iter(jax.random.split(key, 40))

    def nrm(shape, scale):
        return scale * jax.random.normal(next(ks), shape, jnp.float32)

    D, F, L = D_MODEL, D_FF, DEPTH
    NL, NP = N_LRU_LAYERS, N_POOL_LAYERS
    x = nrm((BATCH, SEQ, D), 1.0)
    p = nrm((DEPTH, BATCH, SEQ, PLE_DIM), 1.0)
    ffn1_norm = 1.0 + nrm((L, D), 0.1)
    ffn1_w_gate = nrm((L, D, F), D ** -0.5)
    ffn1_w_up = nrm((L, D, F), D ** -0.5)
    ffn1_w_down = nrm((L, F, D), F ** -0.5)
    mix_norm = 1.0 + nrm((L, D), 0.1)
    lru_w_in = nrm((NL, D, 2 * D_RNN), D ** -0.5)
    lru_conv_w = nrm((NL, CONV_WIDTH, D_RNN), CONV_WIDTH ** -0.5)
    lru_conv_b = nrm((NL, D_RNN), 0.01)
    lru_w_a = nrm((NL, LRU_HEADS, LRU_HEAD_DIM, LRU_HEAD_DIM), LRU_HEAD_DIM ** -0.5)
    lru_b_a = nrm((NL, D_RNN), 0.01)
    lru_w_x = nrm((NL, LRU_HEADS, LRU_HEAD_DIM, LRU_HEAD_DIM), LRU_HEAD_DIM ** -0.5)
    lru_b_x = nrm((NL, D_RNN), 0.01)
    u = jax.random.uniform(next(ks), (NL, D_RNN), jnp.float32, minval=0.9, maxval=0.999)
    a0 = u ** (1.0 / LRU_C)
    lru_a_param = jnp.log(a0) - jnp.log1p(-a0)
    lru_w_out = nrm((NL, D_RNN, D), D_RNN ** -0.5)
    pool_w = nrm((NP, POOL_GROUPS, POOL_GROUP_DIM, POOL_GROUP_DIM), POOL_GROUP_DIM ** -0.5)
    pool_b = nrm((NP, D), 0.01)
    pool_scale = 1.0 + nrm((NP, D), 0.1)
    ffn2_norm = 1.0 + nrm((L, D), 0.1)
    ffn2_w_gate = nrm((L, D, F), D ** -0.5)
    ffn2_w_up = nrm((L, D, F), D ** -0.5)
    ffn2_w_down = nrm((L, F, D), F ** -0.5)
    ple_norm = 1.0 + nrm((L, D), 0.1)
    ple_w_gate = nrm((L, D, D), D ** -0.5)
    ple_w_proj = nrm((L, PLE_DIM, D), PLE_DIM ** -0.5)
    final_norm = 1.0 + nrm((D,), 0.1)
    return {"x": x, "p": p,
            "ffn1_norm": ffn1_norm, "ffn1_w_gate": ffn1_w_gate, "ffn1_w_up": ffn1_w_up, "ffn1_w_down": ffn1_w_down,
            "mix_norm": mix_norm,
            "lru_w_in": lru_w_in, "lru_conv_w": lru_conv_w, "lru_conv_b": lru_conv_b,
            "lru_w_a": lru_w_a, "lru_b_a": lru_b_a, "lru_w_x": lru_w_x, "lru_b_x": lru_b_x,
            "lru_a_param": lru_a_param, "lru_w_out": lru_w_out,
            "pool_w": pool_w, "pool_b": pool_b, "pool_scale": pool_scale,
            "ffn2_norm": ffn2_norm, "ffn2_w_gate": ffn2_w_gate, "ffn2_w_up": ffn2_w_up, "ffn2_w_down": ffn2_w_down,
            "ple_norm": ple_norm, "ple_w_gate": ple_w_gate, "ple_w_proj": ple_w_proj,
            "final_norm": final_norm}


def reference(x, p, ffn1_norm, ffn1_w_gate, ffn1_w_up, ffn1_w_down, mix_norm,
              lru_w_in, lru_conv_w, lru_conv_b, lru_w_a, lru_b_a, lru_w_x, lru_b_x, lru_a_param, lru_w_out,
              pool_w, pool_b, pool_scale,
              ffn2_norm, ffn2_w_gate, ffn2_w_up, ffn2_w_down,
              ple_norm, ple_w_gate, ple_w_proj, final_norm):
    h = x
    for i in range(DEPTH):
        h = h + 0.5 * swiglu(rms_norm(h, ffn1_norm[i]), ffn1_w_gate[i], ffn1_w_up[i], ffn1_w_down[i])
        hn = rms_norm(h, mix_norm[i])
        j = i // N_MIXERS
        if i % N_MIXERS == 0:
            m = rglru_mixer(hn, lru_w_in[j], lru_conv_w[j], lru_conv_b[j], lru_w_a[j], lru_b_a[j],
                            lru_w_x[j], lru_b_x[j], lru_a_param[j], lru_w_out[j])
        else:
            m = pool_mixer(hn, pool_w[j], pool_b[j], pool_scale[j])
        h = h + m
        h = h + 0.5 * swiglu(rms_norm(h, ffn2_norm[i]), ffn2_w_gate[i], ffn2_w_up[i], ffn2_w_down[i])
        gate = jax.nn.sigmoid(rms_norm(h, ple_norm[i]) @ ple_w_gate[i])
        h = h + gate * (p[i].astype(h.dtype) @ ple_w_proj[i])
    return rms_norm(h, final_norm)
```

```python
import numpy as np
from contextlib import ExitStack
import concourse.bass as bass
import concourse.mybir as mybir
from concourse.bass_utils import run_bass_kernel_spmd

F32 = mybir.dt.float32
BF16 = mybir.dt.bfloat16
AF = mybir.ActivationFunctionType
ALU = mybir.AluOpType

PE, ACT, DVE, POOL, SP = "pe", "act", "dve", "pool", "sp"
ENGS = (PE, ACT, DVE, POOL, SP)

NCORE = 8
INTERLEAVE = False
D = 1024
DC = 8
T = 2048
TT = 512
NT = T // TT
FF = 2816
FC = 22
DR = 1280
RC = 10
DEPTH = 4
EPS = 1e-6
NBRS = {0: [0, 1], 1: [0, 1, 2], 2: [1, 2, 3], 3: [2, 3, 4], 4: [3, 4]}
GT_IDX = {}
_n = 0
for _j in range(5):
    for _i in NBRS[_j]:
        GT_IDX[(_j, _i)] = _n
        _n += 1
NGT = _n


class Op:
    __slots__ = ("eng", "fn", "deps", "is_dma", "sem", "target", "seq", "idx")


class Sched:
    def __init__(self, nc):
        self.nc = nc
        self.ops = []
        self.last_w = {}
        self.readers = {}
        self.cnt = {e: 0 for e in ENGS}
        self.dma_sem_total = {}
        self.phase_op = None
        self.last_eng_op = {}
        self.dmas_since = []

    def fence(self, fn):
        o = Op()
        o.eng = DVE
        o.fn = fn
        o.idx = len(self.ops)
        o.deps = set(self.last_eng_op.values()) | set(self.dmas_since)
        o.is_dma = False
        self.cnt[DVE] += 1
        o.seq = self.cnt[DVE]
        o.sem = None
        o.target = None
        self.ops.append(o)
        self.phase_op = o.idx
        self.last_eng_op[DVE] = o.idx
        self.dmas_since = []
        return o

    def _mk(self, eng, fn, reads, writes, nophase=False):
        o = Op()
        o.eng = eng
        o.fn = fn
        o.idx = len(self.ops)
        deps = set()
        if self.phase_op is not None and not nophase:
            deps.add(self.phase_op)
        for k in reads:
            w = self.last_w.get(k)
            if w is not None:
                deps.add(w)
        for k in writes:
            w = self.last_w.get(k)
            if w is not None:
                deps.add(w)
            for r in self.readers.get(k, ()):
                deps.add(r)
        deps.discard(o.idx)
        o.deps = deps
        for k in reads:
            self.readers.setdefault(k, []).append(o.idx)
        for k in writes:
            self.last_w[k] = o.idx
            self.readers[k] = []
        self.ops.append(o)
        return o

    def op(self, eng, fn, reads=(), writes=()):
        o = self._mk(eng, fn, reads, writes)
        o.is_dma = False
        self.cnt[eng] += 1
        o.seq = self.cnt[eng]
        o.sem = None
        o.target = None
        self.last_eng_op[eng] = o.idx
        return o

    def barrier(self, eng, reads):
        o = self._mk(eng, None, reads, ())
        o.is_dma = False
        o.seq = self.cnt[eng]
        o.sem = None
        o.target = None
        return o

    def dma(self, eng, fn, sem, reads=(), writes=(), inc=16, nophase=False):
        o = self._mk(eng, fn, reads, writes, nophase=nophase)
        self.dmas_since.append(o.idx)
        o.is_dma = True
        o.sem = sem
        key = id(sem)
        self.dma_sem_total[key] = self.dma_sem_total.get(key, 0) + inc
        o.target = self.dma_sem_total[key]
        o.seq = inc
        return o

    def emit(self, block, esem):
        ops = self.ops
        per = {e: [o for o in ops if o.eng == e] for e in ENGS}

        def run(eng_name, engine):
            waited = {}
            for o in per[eng_name]:
                need = {}
                for d in o.deps:
                    p = ops[d]
                    if p.is_dma:
                        s, v = p.sem, p.target
                    else:
                        if p.eng == eng_name:
                            if eng_name in (PE, SP):
                                continue
                        s, v = esem[p.eng], p.seq
                    k = id(s)
                    if need.get(k, (None, 0))[1] < v:
                        need[k] = (s, v)
                for k, (s, v) in need.items():
                    if waited.get(k, 0) >= v:
                        continue
                    engine.wait_ge(s, v)
                    waited[k] = v
                if o.fn is None:
                    continue
                ins = o.fn(engine)
                if o.is_dma:
                    ins.then_inc(o.sem, o.seq)
                else:
                    ins.then_inc(esem[eng_name], 1)

        @block.tensor
        def _(e):
            run(PE, e)

        @block.scalar
        def _(e):
            run(ACT, e)

        @block.vector
        def _(e):
            run(DVE, e)

        @block.gpsimd
        def _(e):
            run(POOL, e)

        @block.sync
        def _(e):
            run(SP, e)


VEC_SPECS = [
    ("ffn1_norm", DEPTH * DC), ("mix_norm", DEPTH * DC), ("ffn2_norm", DEPTH * DC), ("ple_norm", DEPTH * DC),
    ("final_norm", DC), ("pool_b", 2 * DC), ("pool_scale", 2 * DC),
    ("lru_conv_w", 2 * 4 * RC), ("lru_conv_b", 2 * RC), ("lru_b_a", 2 * RC), ("lru_b_x", 2 * RC),
    ("lru_a_param", 2 * RC),
    ("sel", 4), ("msk", 4), ("invcnt", 4 * 16),
]
VOFF = {}
_o = 0
for _nm, _n2 in VEC_SPECS:
    VOFF[_nm] = _o
    _o += _n2
NV = _o


def _colpack(v):
    v = np.asarray(v, np.float32)
    C = v.shape[-1] // 128
    lead = int(np.prod(v.shape[:-1])) if v.ndim > 1 else 1
    return np.ascontiguousarray(v.reshape(lead, C, 128).transpose(2, 0, 1).reshape(128, lead * C))


def build(layers=(0, 1, 2, 3), final=True, stop_after=None, phases="1m2p"):
    nc = bass.Bass("TRN2", target_bir_lowering=False)
    dt_in = lambda name, shape: nc.dram_tensor(name, shape, F32, kind="ExternalInput").ap()
    xT_d = dt_in("xT", [D, T])
    pT_d = dt_in("pT", [DEPTH, 256, T])
    vec_d = dt_in("vecs", [128, NV])
    f1g = dt_in("ffn1_w_gate", [DEPTH, D, FF]); f1u = dt_in("ffn1_w_up", [DEPTH, D, FF]); f1d = dt_in("ffn1_w_down", [DEPTH, FF, D])
    f2g = dt_in("ffn2_w_gate", [DEPTH, D, FF]); f2u = dt_in("ffn2_w_up", [DEPTH, D, FF]); f2d = dt_in("ffn2_w_down", [DEPTH, FF, D])
    win_d = dt_in("lru_w_in", [2, D, 2 * DR])
    wout_d = dt_in("lru_w_out", [2, DR, D])
    gt_d = dt_in("lru_gt", [2, 2, 128, 2 * NGT, 128])
    poolw_d = dt_in("pool_w", [2, 4, 256, 256])
    pleg_d = dt_in("ple_w_gate", [DEPTH, D, D])
    plep_d = dt_in("ple_w_proj", [DEPTH, 256, D])
    out_d = nc.dram_tensor("outT", [D, T], F32, kind="ExternalOutput").ap()
    ex_widths = []
    for l_ in layers:
        ex_widths += [128, 128] if l_ % 2 == 0 else [128]
    ex_bufs = [(nc.dram_tensor(f"ex_in{i}", [128, w], F32).ap(), nc.dram_tensor(f"ex_out{i}", [4 * 128, w], F32).ap())
               for i, w in enumerate(ex_widths)]

    with ExitStack() as es:
        H_W = DC * T
        HN_W = DC * T // 2
        BIG_W = 12 * T // 2
        SLOT_W = 2560
        NSLOT = 3
        TMP_W = 7424
        arena = es.enter_context(nc.sbuf_tensor("arena", [128, H_W + HN_W + BIG_W + NSLOT * SLOT_W + TMP_W], F32))
        vec = es.enter_context(nc.sbuf_tensor("vec", [128, NV], F32))
        sm = es.enter_context(nc.sbuf_tensor("sm", [128, 512], F32))
        onesb = es.enter_context(nc.sbuf_tensor("onesb", [128, 128], BF16))
        o0 = 0
        Hreg = arena[:, o0:o0 + H_W]; o0 += H_W
        HNreg = arena[:, o0:o0 + HN_W]; hb0 = o0; o0 += HN_W
        BIGreg = arena[:, o0:o0 + BIG_W]; big0 = o0; o0 += BIG_W
        SLOTreg = [arena[:, o0 + i * SLOT_W: o0 + (i + 1) * SLOT_W] for i in range(NSLOT)]; o0 += NSLOT * SLOT_W
        tmp0 = o0
        Hv = Hreg.rearrange("p (c t) -> p c t", c=DC)
        HNb = HNreg.bitcast(BF16).rearrange("p (c t) -> p c t", c=DC)
        HIDb = BIGreg.bitcast(BF16).rearrange("p (c t) -> p c t", c=12)
        slotb = [s.bitcast(BF16) for s in SLOTreg]

        def tmp_f32(off, n):
            return arena[:, tmp0 + off: tmp0 + off + n]

        def big_f32(off, n):
            return arena[:, big0 + off: big0 + off + n]

        ps = [es.enter_context(nc.psum_tensor(f"ps{i}", [128, TT], F32)) for i in range(8)]
        esem = {e: es.enter_context(nc.semaphore(f"s_{e}")) for e in ENGS}
        ssem = [es.enter_context(nc.semaphore(f"slot{i}")) for i in range(NSLOT)]
        xsem = [es.enter_context(nc.semaphore(f"xs{i}")) for i in range(DC)]
        osem = [es.enter_context(nc.semaphore(f"os{i}")) for i in range(2)]
        vsem = es.enter_context(nc.semaphore("vs"))
        psem = es.enter_context(nc.semaphore("psm"))
        asem = [es.enter_context(nc.semaphore(f"as{i}")) for i in range(2)]
        exsem = [es.enter_context(nc.semaphore(f"ex{i}")) for i in range(3)]
        block = es.enter_context(nc.Block())
        S = Sched(nc)

        V = lambda name, i=0: vec[:, VOFF[name] + i: VOFF[name] + i + 1]

        def tsl(tt):
            return slice(tt * TT, (tt + 1) * TT)

        class Rot:
            def __init__(self, name, aps):
                self.name = name; self.aps = aps; self.i = 0

            def next(self):
                k = self.i % len(self.aps); self.i += 1
                return (self.name, k), self.aps[k]

        psrot = {}

        def psbank(role, banks):
            if role not in psrot:
                psrot[role] = [0, banks]
            r = psrot[role]
            b = r[1][r[0] % len(r[1])]; r[0] += 1
            return ("ps", b), ps[b]

        ring_order = list(range(NSLOT))
        ring_held = []

        def ring_piece(dmas):
            s = ring_order.pop(0); ring_order.append(s)
            key = ("slot", s)
            for dst_fn, src in dmas:
                S.dma(POOL, (lambda e, dst_fn=dst_fn, src=src, s=s: e.dma_start(out=dst_fn(slotb[s]), in_=src)),
                      ssem[s], writes=[key], nophase=True)
            return key, slotb[s]

        def ring_reserve():
            s = ring_order.pop(0)
            ring_held.append(s)
            return ("slot", s), SLOTreg[s]

        def ring_release():
            ring_order.append(ring_held.pop())

        S.dma(SP, lambda e: e.dma_start(out=vec[:], in_=vec_d), vsem, writes=["vec"])
        S.op(POOL, lambda e: e.memset(onesb[:], 1.0), writes=["ones"])
        xv = xT_d.rearrange("(c p) t -> p c t", p=128)
        for c in range(DC):
            S.dma(SP, (lambda e, c=c: e.dma_start(out=Hv[:, c, :], in_=xv[:, c, :])), xsem[c],
                  writes=[("h", c, tt) for tt in range(NT)])

        CNEG = lambda l2, c: sm[:, l2 * RC + c: l2 * RC + c + 1]
        CH = lambda l2, c: sm[:, 20 + l2 * RC + c: 20 + l2 * RC + c + 1]
        PBS = lambda l2, c: sm[:, 40 + l2 * DC + c: 40 + l2 * DC + c + 1]
        HBA = lambda l2, c: sm[:, 56 + l2 * RC + c: 56 + l2 * RC + c + 1]
        HBX = lambda l2, c: sm[:, 76 + l2 * RC + c: 76 + l2 * RC + c + 1]
        ap_ = vec[:, VOFF["lru_a_param"]: VOFF["lru_a_param"] + 2 * RC]
        S.op(ACT, lambda e: e.activation(out=sm[:, 0:20], in_=ap_, func=AF.Exp, scale=-1.0), reads=["vec"], writes=["sm_c"])
        S.op(ACT, lambda e: e.activation(out=sm[:, 0:20], in_=sm[:, 0:20], func=AF.Ln, bias=1.0), reads=["sm_c"], writes=["sm_c"])
        S.op(DVE, lambda e: e.tensor_scalar(out=sm[:, 20:40], in0=sm[:, 0:20], scalar1=-4.0, scalar2=None, op0=ALU.mult), reads=["sm_c"], writes=["sm_c2"])
        S.op(DVE, lambda e: e.tensor_scalar(out=sm[:, 0:20], in0=sm[:, 0:20], scalar1=-8.0, scalar2=None, op0=ALU.mult), reads=["sm_c", "sm_c2"], writes=["sm_c"])
        S.op(DVE, lambda e: e.tensor_tensor(out=sm[:, 40:56], in0=vec[:, VOFF["pool_b"]:VOFF["pool_b"] + 16],
                                            in1=vec[:, VOFF["pool_scale"]:VOFF["pool_scale"] + 16], op=ALU.mult), reads=["vec"], writes=["sm_pbs"])
        S.op(DVE, lambda e: e.tensor_scalar(out=sm[:, 56:76], in0=vec[:, VOFF["lru_b_a"]:VOFF["lru_b_a"] + 20], scalar1=0.5, scalar2=None, op0=ALU.mult), reads=["vec"], writes=["sm_hb"])
        S.op(DVE, lambda e: e.tensor_scalar(out=sm[:, 76:96], in0=vec[:, VOFF["lru_b_x"]:VOFF["lru_b_x"] + 20], scalar1=0.5, scalar2=None, op0=ALU.mult), reads=["vec", "sm_hb"], writes=["sm_hb"])

        hkeys = lambda c, tt: ("h", c, tt)

        def fence():
            S.fence(lambda e: e.memset(sm[:, 500:501], 0.0))

        sq_rot = Rot("sq", [tmp_f32(i * 256, 256).bitcast(BF16) for i in range(2)])
        rs_rot = Rot("rs", [tmp_f32(512 + i * 512, 512) for i in range(2)])
        TMPB = 512 + 1024

        def norm(gname, gidx, dst_fn, dst_key_fn, tts=range(NT)):
            for tt in tts:
                pk, pb = psbank("norm", [6, 7])
                for c in range(DC):
                    sk, sq = sq_rot.next()
                    S.op(ACT, (lambda e, sq=sq, c=c, tt=tt: e.activation(out=sq, in_=Hv[:, c, tsl(tt)], func=AF.Square)),
                         reads=[hkeys(c, tt)], writes=[sk])
                    S.op(PE, (lambda e, sq=sq, pb=pb, c=c: e.matmul(pb[:], lhsT=onesb[:], rhs=sq, start=(c == 0), stop=(c == DC - 1))),
                         reads=[sk, "ones"], writes=[pk])
                rk, rs = rs_rot.next()
                S.op(ACT, (lambda e, rs=rs, pb=pb: e.activation(out=rs, in_=pb[:], func=AF.Ln, scale=1.0 / D, bias=V("eps"))),
                     reads=[pk, "vec"], writes=[rk])
                S.op(ACT, (lambda e, rs=rs: e.activation(out=rs, in_=rs, func=AF.Exp, scale=-0.5)), reads=[rk], writes=[rk])
                for c in range(DC):
                    S.op(DVE, (lambda e, rs=rs, c=c, tt=tt: e.scalar_tensor_tensor(
                        out=dst_fn(c, tt), in0=Hv[:, c, tsl(tt)], scalar=V(gname, gidx * DC + c), in1=rs, op0=ALU.mult, op1=ALU.mult)),
                         reads=[hkeys(c, tt), rk, "vec"], writes=[dst_key_fn(c, tt)])

        hn_dst = lambda c, tt: HNb[:, c, tsl(tt)]
        hn_key = lambda c, tt: ("hn", c, tt)

        sg_rot = Rot("sg", [tmp_f32(TMPB + i * 512, 512) for i in range(3)])
        FFN_GROUPS = [list(range(0, 12)), list(range(12, 22))]

        def ffn(l, wg, wu, wd, nname, mid_hook=None):
            normed = set()

            def need(tt):
                for t_ in (tt, tt + 1):
                    if t_ < NT and t_ not in normed:
                        normed.add(t_)
                        norm(nname, l, hn_dst, hn_key, tts=(t_,))
            wgv = wg[l].rearrange("(k p) f -> p k f", p=128)
            wuv = wu[l].rearrange("(k p) f -> p k f", p=128)
            for gi_, grp in enumerate(FFN_GROUPS):
                if gi_ == 1 and mid_hook is not None:
                    mid_hook()
                for pi in range(0, len(grp), 2):
                    f0 = grp[pi]
                    key, sl = ring_piece([
                        (lambda s: s[:, 0:2048].rearrange("p (k f) -> p k f", k=8), wgv[:, :, f0 * 128:(f0 + 2) * 128]),
                        (lambda s: s[:, 2048:4096].rearrange("p (k f) -> p k f", k=8), wuv[:, :, f0 * 128:(f0 + 2) * 128]),
                    ])
                    wgs = sl[:, 0:2048].rearrange("p (k f) -> p k f", k=8)
                    wus = sl[:, 2048:4096].rearrange("p (k f) -> p k f", k=8)
                    for fo in range(2):
                        fi = pi + fo
                        for tt in range(NT):
                            need(tt)
                            gk, gb = psbank("ffg", [0, 1])
                            uk, ub = psbank("ffu", [2, 3])
                            for k in range(DC):
                                S.op(PE, (lambda e, gb=gb, k=k, fo=fo, tt=tt, wgs=wgs: e.matmul(
                                    gb[:], lhsT=wgs[:, k, fo * 128:(fo + 1) * 128], rhs=HNb[:, k, tsl(tt)], start=(k == 0), stop=(k == DC - 1))),
                                     reads=[key, hn_key(k, tt)], writes=[gk])
                            for k in range(DC):
                                S.op(PE, (lambda e, ub=ub, k=k, fo=fo, tt=tt, wus=wus: e.matmul(
                                    ub[:], lhsT=wus[:, k, fo * 128:(fo + 1) * 128], rhs=HNb[:, k, tsl(tt)], start=(k == 0), stop=(k == DC - 1))),
                                     reads=[key, hn_key(k, tt)], writes=[uk])
                            sk, sg = sg_rot.next()
                            S.op(ACT, (lambda e, sg=sg, gb=gb: e.activation(out=sg, in_=gb[:], func=AF.Silu)), reads=[gk], writes=[sk])
                            S.op(DVE, (lambda e, sg=sg, ub=ub, fi=fi, tt=tt: e.tensor_tensor(out=HIDb[:, fi, tsl(tt)], in0=sg, in1=ub[:], op=ALU.mult)),
                                 reads=[sk, uk], writes=[("hid", fi, tt)])
                nf = len(grp)
                wdv = wd[l][grp[0] * 128:(grp[0] + nf) * 128, :].rearrange("(k p) d -> p k d", p=128)
                for jp in range(0, DC, 2):
                    key, sl = ring_piece([(lambda s, nf=nf: s[:, 0:nf * 256].rearrange("p (k d) -> p k d", k=nf), wdv[:, :, jp * 128:(jp + 2) * 128])])
                    wds = sl[:, 0:nf * 256].rearrange("p (k d) -> p k d", k=nf)
                    for jo in range(2):
                        j = jp + jo
                        for tt in range(NT):
                            dk, db = psbank("ffd", [4, 5])
                            for fi in range(nf):
                                S.op(PE, (lambda e, db=db, fi=fi, jo=jo, tt=tt, wds=wds, nf=nf: e.matmul(
                                    db[:], lhsT=wds[:, fi, jo * 128:(jo + 1) * 128], rhs=HIDb[:, fi, tsl(tt)], start=(fi == 0), stop=(fi == nf - 1))),
                                     reads=[key, ("hid", fi, tt)], writes=[dk])
                            S.op(DVE, (lambda e, db=db, j=j, tt=tt: e.scalar_tensor_tensor(
                                out=Hv[:, j, tsl(tt)], in0=db[:], scalar=0.5, in1=Hv[:, j, tsl(tt)], op0=ALU.mult, op1=ALU.add)),
                                 reads=[dk, hkeys(j, tt)], writes=[hkeys(j, tt)])

        ple_bufs = {}

        def ple_prefetch(l):
            PTb = tmp_f32(3072, 2048).bitcast(BF16).rearrange("p (c t) -> p c t", c=2)
            wps = tmp_f32(5120, 1024).bitcast(BF16).rearrange("p (k d) -> p k d", k=2)
            S.dma(POOL, lambda e: e.dma_start(out=PTb, in_=pT_d[l].rearrange("(c p) t -> p c t", p=128)), psem, writes=["pt"])
            S.dma(POOL, lambda e: e.dma_start(out=wps, in_=plep_d[l].rearrange("(k p) d -> p k d", p=128)), asem[0], writes=["wproj"])
            ple_bufs[l] = (PTb, wps, "wproj")

        def ple(l):
            normed = set()

            def need(tt):
                for t_ in (tt, tt + 1):
                    if t_ < NT and t_ not in normed:
                        normed.add(t_)
                        norm("ple_norm", l, hn_dst, hn_key, tts=(t_,))
            PTb, wps, kp = ple_bufs[l]
            wgv = pleg_d[l].rearrange("(k p) d -> p k d", p=128)
            pg_rot = Rot("pg", [tmp_f32(TMPB + i * 512, 512) for i in range(3)])
            for jp in range(0, DC, 2):
                key, sl = ring_piece([(lambda s: s[:, 0:2048].rearrange("p (k d) -> p k d", k=8), wgv[:, :, jp * 128:(jp + 2) * 128])])
                wgs = sl[:, 0:2048].rearrange("p (k d) -> p k d", k=8)
                for jo in range(2):
                    j = jp + jo
                    for tt in range(NT):
                        need(tt)
                        ak, ab = psbank("ffg", [0, 1])
                        bk, bb = psbank("ffu", [2, 3])
                        for k in range(DC):
                            S.op(PE, (lambda e, ab=ab, k=k, jo=jo, tt=tt, wgs=wgs: e.matmul(
                                ab[:], lhsT=wgs[:, k, jo * 128:(jo + 1) * 128], rhs=HNb[:, k, tsl(tt)], start=(k == 0), stop=(k == DC - 1))),
                                 reads=[key, hn_key(k, tt)], writes=[ak])
                        for k in range(2):
                            S.op(PE, (lambda e, bb=bb, k=k, j=j, tt=tt: e.matmul(
                                bb[:], lhsT=wps[:, k, j * 128:(j + 1) * 128], rhs=PTb[:, k, tsl(tt)], start=(k == 0), stop=(k == 1))),
                                 reads=[kp, "pt"], writes=[bk])
                        gk, g = pg_rot.next()
                        S.op(ACT, (lambda e, g=g, ab=ab: e.activation(out=g, in_=ab[:], func=AF.Sigmoid)), reads=[ak], writes=[gk])
                        S.op(DVE, (lambda e, g=g, bb=bb: e.tensor_tensor(out=g, in0=g, in1=bb[:], op=ALU.mult)), reads=[gk, bk], writes=[gk])
                        S.op(POOL, (lambda e, g=g, j=j, tt=tt: e.tensor_tensor(out=Hv[:, j, tsl(tt)], in0=Hv[:, j, tsl(tt)], in1=g, op=ALU.add)),
                             reads=[gk, hkeys(j, tt)], writes=[hkeys(j, tt)])

        ex_cnt = {"i": 0}

        def exchange(src_ap, width, src_keys, dst_ap, dst_key):
            i = ex_cnt["i"]; ex_cnt["i"] += 1
            exi, exo = ex_bufs[i]
            assert exi.shape[1] == width, (exi.shape, width)
            S.dma(POOL, lambda e: e.dma_start(out=exi, in_=src_ap), exsem[0], reads=src_keys, writes=[("exin", i)])
            S.dma(POOL, lambda e: e.collective_compute("AllGather", ALU.bypass, replica_groups=[[0, 1, 2, 3], [4, 5, 6, 7]],
                                                       ins=[exi], outs=[exo]), exsem[1], reads=[("exin", i)], writes=[("exout", i)], inc=1)
            S.dma(POOL, lambda e: e.dma_start(out=dst_ap, in_=exo.rearrange("(r p) f -> p r f", p=128)), exsem[2],
                  reads=[("exout", i)], writes=[dst_key])

        def select_prev(dst_ap, g_ap, width, dst_keys, g_key, eng=DVE):
            S.op(DVE, lambda e: e.tensor_scalar(out=dst_ap, in0=g_ap[:, 0, :], scalar1=V("sel", 0), scalar2=None, op0=ALU.mult),
                 reads=[g_key, "vec"], writes=dst_keys)
            for j in range(1, 4):
                S.op(DVE, (lambda e, j=j: e.scalar_tensor_tensor(out=dst_ap, in0=g_ap[:, j, :], scalar=V("sel", j), in1=dst_ap, op0=ALU.mult, op1=ALU.add)),
                     reads=[g_key, "vec"] + dst_keys, writes=dst_keys)

        def pool_layer(l):
            l2 = l // 2
            HW = 16 + T
            HN32 = arena[:, hb0: hb0 + DC * HW].rearrange("p (c t) -> p c t", c=DC)
            extra0 = hb0 + DC * HW - big0
            EXS = big_f32(extra0, 128)
            EXG = big_f32(extra0 + 128, 512).rearrange("p (r f) -> p r f", r=4)
            S.op(POOL, lambda e: e.memset(EXS, 0.0), writes=["pexs"])
            hn32_key = lambda c, tt: ("hn32", c, tt)
            norm("mix_norm", l, lambda c, tt: HN32[:, c, 16 + tt * TT: 16 + (tt + 1) * TT], hn32_key, tts=(3, 0, 1, 2))
            for c in range(DC):
                S.op(POOL, (lambda e, c=c: e.tensor_copy(EXS[:, c * 15:(c + 1) * 15], HN32[:, c, 16 + T - 15: 16 + T])),
                     reads=[hn32_key(c, 3)], writes=["pexs"])
            exchange(EXS, 128, ["pexs"], EXG, "pexg")
            HALO = big_f32(extra0 + 640, 120)
            select_prev(HALO, EXG[:, :, 0:120], 120, ["phalo"], "pexg")
            for c in range(DC):
                S.op(POOL, (lambda e, c=c: e.tensor_copy(HN32[:, c, 1:16], HALO[:, c * 15:(c + 1) * 15])), reads=["phalo"], writes=[("hn32h", c)])
            kw, slw = ring_piece([(lambda s: s[:, 0:2048].rearrange("p (g k d) -> p g k d", g=4, k=2),
                                   poolw_d[l2].rearrange("g (k p) d -> p g k d", p=128))])
            pw = slw[:, 0:2048].rearrange("p (g k d) -> p g k d", g=4, k=2)
            WB = 528
            wk_rot = Rot("pwk", [tmp_f32(TMPB + i * WB, WB) for i in range(3)])
            U_rot = [tmp_f32(TMPB + 3 * WB + i * 2048, 2048).bitcast(BF16).rearrange("p (c t) -> p c t", c=DC) for i in range(2)]
            assert TMPB + 3 * WB + 2 * 2048 <= TMP_W
            pt_rot = Rot("ptmp", [big_f32(extra0 + 1024 + i * 512, 512) for i in range(3)])
            wcnt = [0]
            for tt in range(NT):
                Ub = U_rot[tt % 2]
                ukey = lambda c: ("pu", tt % 2, c)
                for c in range(DC):
                    g = c // 2
                    win = 2 << g
                    base = tt * TT
                    src = HN32[:, c, base: base + WB]
                    rdk = [hn32_key(c, tt), ("hn32h", c)] + ([hn32_key(c, tt - 1)] if tt > 0 else [])
                    cur = src; curk = rdk; sh = 1
                    while sh < win:
                        wkk, wk = wk_rot.next()
                        lo = 2 * sh
                        wcnt[0] += 1
                        S.op(POOL if wcnt[0] % 3 == 2 else DVE,
                             (lambda e, wk=wk, cur=cur, sh=sh, lo=lo: e.tensor_tensor(out=wk[:, lo:WB], in0=cur[:, lo:WB], in1=cur[:, lo - sh:WB - sh], op=ALU.add)),
                             reads=curk, writes=[wkk])
                        cur = wk; curk = [wkk]; sh *= 2
                    S.op(DVE, (lambda e, cur=cur, c=c, win=win, src=src, Ub=Ub: e.scalar_tensor_tensor(
                        out=Ub[:, c, :], in0=cur[:, 16:WB], scalar=1.0 / win, in1=src[:, 16:WB], op0=ALU.mult, op1=ALU.subtract)),
                         reads=curk + rdk, writes=[ukey(c)])
                    if tt == 0:
                        wkk2, wk2 = wk_rot.next()
                        S.op(DVE, (lambda e, cur=cur, wk2=wk2, g=g: e.tensor_tensor(out=wk2[:, 0:16], in0=cur[:, 16:32],
                                                                                     in1=vec[:, VOFF["invcnt"] + g * 16: VOFF["invcnt"] + (g + 1) * 16], op=ALU.mult)),
                             reads=curk + ["vec"], writes=[wkk2])
                        S.op(DVE, (lambda e, wk2=wk2, c=c, src=src, Ub=Ub: e.tensor_tensor(out=Ub[:, c, 0:16], in0=wk2[:, 0:16], in1=src[:, 16:32], op=ALU.subtract)),
                             reads=[wkk2] + rdk, writes=[ukey(c)])
                for g in range(4):
                    for jo in range(2):
                        j = 2 * g + jo
                        pk_, pb_ = psbank("ffd", [4, 5])
                        for k in range(2):
                            S.op(PE, (lambda e, pb_=pb_, g=g, k=k, jo=jo, Ub=Ub: e.matmul(
                                pb_[:], lhsT=pw[:, g, k, jo * 128:(jo + 1) * 128], rhs=Ub[:, 2 * g + k, :], start=(k == 0), stop=(k == 1))),
                                 reads=[kw, ukey(2 * g + k)], writes=[pk_])
                        tk, tp = pt_rot.next()
                        S.op(ACT, (lambda e, tp=tp, pb_=pb_, j=j: e.activation(out=tp, in_=pb_[:], func=AF.Identity,
                                                                               scale=V("pool_scale", l2 * DC + j), bias=PBS(l2, j))),
                             reads=[pk_, "vec", "sm_pbs"], writes=[tk])
                        S.op(POOL, (lambda e, tp=tp, j=j, tt=tt: e.tensor_tensor(out=Hv[:, j, tsl(tt)], in0=Hv[:, j, tsl(tt)], in1=tp, op=ALU.add)),
                             reads=[tk, hkeys(j, tt)], writes=[hkeys(j, tt)])
            allk = [hn32_key(c, tt) for c in range(DC) for tt in range(NT)] + [("hn32h", c) for c in range(DC)] + ["pexs", "pexg", "phalo"]
            return allk

        def lru_layer(l):
            l2 = l // 2
            norm("mix_norm", l, hn_dst, hn_key)
            winx = [BIGreg[:, hf * 2560:(hf + 1) * 2560].bitcast(BF16).rearrange("p (k f) -> p k f", k=8) for hf in range(2)]
            GT = BIGreg[:, 5120:5120 + 1664].bitcast(BF16).rearrange("p (n f) -> p n f", n=2 * NGT)
            gts = [GT, GT]
            winv = win_d[l2].rearrange("(k p) f -> p k f", p=128)
            for hf in range(2):
                S.dma(POOL, (lambda e, hf=hf: e.dma_start(out=winx[hf], in_=winv[:, :, DR + hf * 640: DR + (hf + 1) * 640])), (asem[0], psem)[hf],
                      writes=[("A", hf)])

            def load_gt(hf):
                S.dma(POOL, (lambda e: e.dma_start(out=GT, in_=gt_d[l2, hf])), asem[1], writes=["gt"])
            b0 = 5120 + 1664
            XCb = [big_f32(b0 + i * 1280, 1280).bitcast(BF16).rearrange("p (c t) -> p c t", c=5) for i in range(2)]
            b0 += 2560 - 1280
            Yb = big_f32(b0 + 1280, 1280).bitcast(BF16).rearrange("p (c t) -> p c t", c=5)
            smallb = b0 + 2560
            HALO2 = [big_f32(smallb + i * 32, 30).rearrange("p (c t) -> p c t", c=RC) for i in range(2)]
            HALO0f = big_f32(smallb + 64, 30)
            STATE = big_f32(smallb + 96, RC)
            TSUM = big_f32(smallb + 112, RC * NT).rearrange("p (c t) -> p c t", c=RC)
            HC = [big_f32(smallb + 160 + i * 4, 3) for i in range(10)]
            HCT = [big_f32(smallb + 200 + i * 4, 3) for i in range(10)]
            HINIT = big_f32(smallb + 448, RC)
            TSM = big_f32(smallb + 464, RC)
            RS1 = big_f32(smallb + 480, RC)
            ykeys = [("y", ci) for ci in range(5)]
            EXS_ = big_f32(b0 + 1280, 128)
            EXG_ = big_f32(b0 + 1280 + 128, 512).rearrange("p (r f) -> p r f", r=4)
            EX1 = EXS_; EX1G = EXG_; EX2 = EXS_; EX2G = EXG_
            S.op(POOL, lambda e: e.memset(EXS_, 0.0), writes=["ex1", "ex2a", "ex2b"] + ykeys)
            XC2 = [tmp_f32(i * 2560, 2560).rearrange("p (c t) -> p c t", c=5) for i in range(2)]
            sk_, sslot = ring_reserve()
            tiles = [tmp_f32(5120 + i * 512, 512) for i in range((TMP_W - 5120) // 512)] + \
                    [big_f32(b0 + 3072 + i * 512, 512) for i in range((BIG_W - b0 - 3072) // 512)] + \
                    [sslot[:, i * 512:(i + 1) * 512] for i in range(SLOT_W // 512)]
            assert len(tiles) >= 10, len(tiles)
            extra = [ring_reserve(), ring_reserve()]
            tiles_p1 = tiles + [sl_[:, i * 512:(i + 1) * 512] for _, sl_ in extra for i in range(SLOT_W // 512)]
            tl_box = [Rot("tl", tiles_p1)]
            tlkeys = [("tl", i) for i in range(len(tiles))]
            xkeys = [k_ for k_, _ in extra] + [("tl", i) for i in range(len(tiles), len(tiles_p1))]

            pk3, pb3 = psbank("ffd", [4, 5])
            for c in range(RC):
                hf, ci = divmod(c, 5)
                for k in range(DC):
                    S.op(PE, (lambda e, c=c, hf=hf, ci=ci, k=k: e.matmul(pb3[:, c * 4: c * 4 + 3], lhsT=winx[hf][:, k, ci * 128:(ci + 1) * 128],
                                                                      rhs=HNb[:, k, T - 3:T], start=(k == 0), stop=(k == DC - 1))),
                         reads=[("A", hf), hn_key(k, 3)], writes=[pk3])
            S.op(ACT, lambda e: e.activation(out=EX1[:, 0:30].rearrange("p (c t) -> p c t", c=RC), in_=pb3[:, 0:40].rearrange("p (c t) -> p c t", c=RC)[:, :, 0:3], func=AF.Identity),
                 reads=[pk3], writes=["ex1"])
            exchange(EX1, 128, ["ex1"], EX1G, "ex1g")
            select_prev(HALO0f, EX1G[:, :, 0:30], 30, ["halo0"], "ex1g")
            S.op(DVE, lambda e: e.memset(sm[:, 502:503], 0.0),
                 writes=[("sq", 0), ("sq", 1), ("rs", 0), ("rs", 1), sk_] + [("xc", b_, ci) for b_ in range(2) for ci in range(5)] + tlkeys + xkeys)

            cw = lambda kk, c: V("lru_conv_w", l2 * 4 * RC + kk * RC + c)
            hc_i = [0]

            def stage1(u, hf, tt):
                wk = ("A", hf)
                XC = XC2[u % 2]; XCbb = XCb[u % 2]
                Hin = HALO2[tt % 2]; Hout = HALO2[(tt + 1) % 2]
                for ci in range(5):
                    c = hf * 5 + ci
                    xk, xps = psbank("xg3", [0, 1, 5]) if cur_pass[0] == 2 else psbank("xps4", [0, 1, 6, 7])
                    for k in range(DC):
                        S.op(PE, (lambda e, xps=xps, k=k, ci=ci: e.matmul(xps[:], lhsT=winx[hf][:, k, ci * 128:(ci + 1) * 128], rhs=HNb[:, k, tsl(tt)],
                                                                        start=(k == 0), stop=(k == DC - 1))),
                             reads=[wk, hn_key(k, tt)], writes=[xk])
                    hi = hc_i[0] % 10; hc_i[0] += 1
                    hc, hct = HC[hi], HCT[hi]
                    hk = ("hc", hi)
                    S.op(POOL, (lambda e, hc=hc, c=c: e.tensor_scalar(out=hc[:, 0:3], in0=Hin[:, c, 0:3], scalar1=cw(0, c), scalar2=None, op0=ALU.mult)),
                         reads=[("halo", tt % 2, c), "vec"], writes=[hk])
                    S.op(POOL, (lambda e, hct=hct, c=c: e.tensor_scalar(out=hct[:, 0:2], in0=Hin[:, c, 1:3], scalar1=cw(1, c), scalar2=None, op0=ALU.mult)),
                         reads=[("halo", tt % 2, c), "vec"], writes=[("hct", hi)])
                    S.op(POOL, (lambda e, hc=hc, hct=hct: e.tensor_tensor(out=hc[:, 0:2], in0=hc[:, 0:2], in1=hct[:, 0:2], op=ALU.add)),
                         reads=[hk, ("hct", hi)], writes=[hk])
                    S.op(POOL, (lambda e, hct=hct, c=c: e.tensor_scalar(out=hct[:, 0:1], in0=Hin[:, c, 2:3], scalar1=cw(2, c), scalar2=None, op0=ALU.mult)),
                         reads=[("halo", tt % 2, c), "vec", hk], writes=[("hct", hi)])
                    S.op(POOL, (lambda e, hc=hc, hct=hct: e.tensor_tensor(out=hc[:, 0:1], in0=hc[:, 0:1], in1=hct[:, 0:1], op=ALU.add)),
                         reads=[hk, ("hct", hi)], writes=[hk])
                    S.op(ACT, (lambda e, xps=xps, ci=ci, c=c, XC=XC: e.activation(out=XC[:, ci, :], in_=xps[:], func=AF.Identity, scale=cw(3, c),
                                                                                bias=V("lru_conv_b", l2 * RC + c))),
                         reads=[xk, "vec"], writes=[("xc", u % 2, ci)])
                    S.op(ACT, (lambda e, xps=xps, c=c, Hout=Hout: e.activation(out=Hout[:, c, :], in_=xps[:, TT - 3:TT], func=AF.Identity)),
                         reads=[xk], writes=[("halo", (tt + 1) % 2, c)])
                    for kk in (2, 1, 0):
                        sh = 3 - kk
                        S.op(DVE, (lambda e, xps=xps, ci=ci, c=c, kk=kk, sh=sh, XC=XC: e.scalar_tensor_tensor(
                            out=XC[:, ci, sh:TT], in0=xps[:, 0:TT - sh], scalar=cw(kk, c), in1=XC[:, ci, sh:TT], op0=ALU.mult, op1=ALU.add)),
                             reads=[xk, "vec", ("xc", u % 2, ci)], writes=[("xc", u % 2, ci)])
                    S.op(DVE, (lambda e, ci=ci, hc=hc, XC=XC: e.tensor_tensor(out=XC[:, ci, 0:3], in0=XC[:, ci, 0:3], in1=hc[:, 0:3], op=ALU.add)),
                         reads=[hk, ("xc", u % 2, ci)], writes=[("xc", u % 2, ci)])
                    S.op(POOL, (lambda e, ci=ci, XC=XC, XCbb=XCbb: e.tensor_copy(XCbb[:, ci, :], XC[:, ci, :])), reads=[("xc", u % 2, ci)], writes=[("xcb", u % 2, ci)])
                    yield

            def stage2(u, pas, hf, tt, wing=None, wingk=None, wouts=None, woutk=None):
                wk = ("A", hf)
                XC = XC2[u % 2]; XCbb = XCb[u % 2]
                for grp in (([0, 1], [2, 3], [4]) if pas == 2 else ([0, 1, 2], [3, 4])):
                    tl = {}
                    for cj in grp:
                        c = hf * 5 + cj
                        if pas == 2:
                            rk, r_ps = psbank("gate3", [2, 3, 4])
                            ik, i_ps = psbank("gate3", [2, 3, 4])
                        else:
                            rk, r_ps = psbank("ffu", [2, 3])
                            ik, i_ps = psbank("ffd", [4, 5])
                        nb = NBRS[cj]
                        for gi, (pk_, pb_) in enumerate(((rk, r_ps), (ik, i_ps))):
                            for n_, i in enumerate(nb):
                                S.op(PE, (lambda e, pb_=pb_, gi=gi, i=i, cj=cj, n_=n_, nb=nb, XCbb=XCbb: e.matmul(
                                    pb_[:], lhsT=gts[hf][:, gi * NGT + GT_IDX[(cj, i)], :], rhs=XCbb[:, i, :], start=(n_ == 0), stop=(n_ == len(nb) - 1))),
                                     reads=["gt", ("xcb", u % 2, i)], writes=[pk_])
                        Mk, M = tl_box[0].next()
                        Ik, I = tl_box[0].next()
                        Ak, A = tl_box[0].next()
                        tl[cj] = (Mk, M, Ik, I, Ak, A)
                        kw = dict(accum_out=TSUM[:, c, tt:tt + 1]) if pas == 1 else {}
                        S.op(ACT, (lambda e, M=M, r_ps=r_ps, c=c, kw=kw: e.activation(out=M, in_=r_ps[:], func=AF.Tanh, scale=0.5, bias=HBA(l2, c), **kw)),
                             reads=[rk, "sm_hb"], writes=[Mk] + ([("tsum", c)] if pas == 1 else []))
                        S.op(ACT, (lambda e, I=I, i_ps=i_ps, c=c: e.activation(out=I, in_=i_ps[:], func=AF.Tanh, scale=0.5, bias=HBX(l2, c))),
                             reads=[ik, "sm_hb"], writes=[Ik])
                        S.op(ACT, (lambda e, A=A, M=M, c=c: e.activation(out=A, in_=M, func=AF.Exp, scale=CH(l2, c), bias=CH(l2, c))),
                             reads=[Mk, "sm_c"], writes=[Ak])
                        S.op(ACT, (lambda e, M=M, c=c: e.activation(out=M, in_=M, func=AF.Exp, scale=CNEG(l2, c), bias=CNEG(l2, c))),
                             reads=[Mk, "sm_c"], writes=[Mk])
                        S.op(DVE, (lambda e, I=I, cj=cj, XC=XC: e.scalar_tensor_tensor(out=I, in0=I, scalar=1.0, in1=XC[:, cj, :], op0=ALU.add, op1=ALU.mult)),
                             reads=[Ik, ("xc", u % 2, cj)], writes=[Ik])
                        yield
                    for cj in grp:
                        Mk, M, Ik, I, Ak, A = tl[cj]
                        S.op(ACT, (lambda e, M=M: e.activation(out=M, in_=M, func=AF.Sqrt, scale=-1.0, bias=1.0)), reads=[Mk], writes=[Mk])
                    for cj in grp:
                        c = hf * 5 + cj
                        Mk, M, Ik, I, Ak, A = tl[cj]
                        S.op(DVE, (lambda e, I=I, M=M: e.scalar_tensor_tensor(out=I, in0=I, scalar=0.5, in1=M, op0=ALU.mult, op1=ALU.mult)),
                             reads=[Ik, Mk], writes=[Ik])
                        S.op(DVE, (lambda e, M=M, A=A, I=I, c=c: e.tensor_tensor_scan(out=M, data0=A, data1=I, initial=STATE[:, c:c + 1], op0=ALU.mult, op1=ALU.add)),
                             reads=[Ak, Ik, ("state", c), Mk], writes=[Mk])
                        S.op(POOL, (lambda e, M=M, c=c: e.tensor_copy(STATE[:, c:c + 1], M[:, TT - 1:TT])), reads=[Mk], writes=[("state", c)])
                    if pas == 2:
                        for cj in grp:
                            Mk, M, Ik, I, Ak, A = tl[cj]
                            gk_, g_ps = psbank("xg3", [0, 1, 5])
                            for k in range(DC):
                                S.op(PE, (lambda e, g_ps=g_ps, k=k, cj=cj: e.matmul(g_ps[:], lhsT=wing[:, k, cj * 128:(cj + 1) * 128], rhs=HNb[:, k, tsl(tt)],
                                                                                  start=(k == 0), stop=(k == DC - 1))),
                                     reads=[wingk, hn_key(k, tt)], writes=[gk_])
                            S.op(ACT, (lambda e, A=A, g_ps=g_ps: e.activation(out=A, in_=g_ps[:], func=AF.Gelu_apprx_tanh)), reads=[gk_, Ak], writes=[Ak])
                        for cj in grp:
                            Mk, M, Ik, I, Ak, A = tl[cj]
                            S.op(POOL, (lambda e, A=A, M=M, cj=cj: e.tensor_tensor(out=Yb[:, cj, :], in0=M, in1=A, op=ALU.mult)), reads=[Ak, Mk], writes=[("y", cj)])
                if pas == 2:
                    for j in range(DC):
                        ok_, o_ps = psbank("norm", [6, 7])
                        for ci in range(5):
                            S.op(PE, (lambda e, o_ps=o_ps, ci=ci, j=j: e.matmul(o_ps[:], lhsT=wouts[:, ci, j * 128:(j + 1) * 128], rhs=Yb[:, ci, :],
                                                                              start=(ci == 0), stop=(ci == 4))),
                                 reads=[woutk, ("y", ci)], writes=[ok_])
                        S.op(DVE, (lambda e, o_ps=o_ps, j=j: e.tensor_tensor(out=Hv[:, j, tsl(tt)], in0=o_ps[:], in1=Hv[:, j, tsl(tt)], op=ALU.add)),
                             reads=[ok_, hkeys(j, tt)], writes=[hkeys(j, tt)])

            def reset_run(init_state_ap):
                S.op(POOL, lambda e: e.tensor_copy(HALO2[0].rearrange("p c t -> p (c t)") if False else big_f32(smallb, 30), HALO0f), reads=["halo0"],
                     writes=[("halo", 0, c) for c in range(RC)])
                if init_state_ap is None:
                    S.op(POOL, lambda e: e.memset(STATE, 0.0), writes=[("state", c) for c in range(RC)])
                else:
                    S.op(POOL, lambda e: e.tensor_copy(STATE, init_state_ap), reads=["hinit"], writes=[("state", c) for c in range(RC)])

            cur_pass = [1]

            def run_pass(pas, wts):
                cur_pass[0] = pas
                units = [(hf, tt) for hf in range(2) for tt in range(NT)]
                for _ in stage1(0, *units[0]):
                    pass
                for u, (hf, tt) in enumerate(units):
                    if tt == 0:
                        load_gt(hf)
                        w_ = wts(hf) if pas == 2 else (None,) * 4
                    g1 = stage1(u + 1, *units[u + 1]) if u + 1 < len(units) else iter(())
                    if not INTERLEAVE:
                        for _ in g1:
                            pass
                    for _ in stage2(u, pas, hf, tt, *w_):
                        next(g1, None)
                    for _ in g1:
                        pass

            reset_run(None)
            run_pass(1, None)
            allts = [("tsum", c) for c in range(RC)]
            S.op(DVE, lambda e: e.tensor_reduce(out=RS1, in_=TSUM, axis=mybir.AxisListType.X, op=ALU.add), reads=allts, writes=["rs1"])
            S.op(DVE, lambda e: e.scalar_tensor_tensor(out=RS1, in0=RS1, scalar=float(T), in1=sm[:, 20 + l2 * RC: 20 + (l2 + 1) * RC], op0=ALU.add, op1=ALU.mult),
                 reads=["rs1", "sm_c"], writes=["rs1"])
            S.op(ACT, lambda e: e.activation(out=EX2[:, 0:RC], in_=RS1, func=AF.Exp), reads=["rs1", "ex1g", "halo0"], writes=["ex2a", "ex1"])
            S.op(POOL, lambda e: e.tensor_copy(EX2[:, RC:2 * RC], STATE), reads=[("state", c) for c in range(RC)] + ["ex1g", "halo0"], writes=["ex2b", "ex1"])
            exchange(EX2, 128, ["ex2a", "ex2b"], EX2G, "ex1g")
            S.op(DVE, lambda e: e.memset(HINIT, 0.0), writes=["hinit"])
            for j in range(3):
                S.op(DVE, (lambda e, j=j: e.tensor_tensor(out=TSM, in0=EX2G[:, j, 0:RC], in1=HINIT, op=ALU.mult)), reads=["ex1g", "hinit"], writes=["tsm"])
                S.op(DVE, (lambda e, j=j: e.tensor_tensor(out=TSM, in0=TSM, in1=EX2G[:, j, RC:2 * RC], op=ALU.add)), reads=["ex1g", "tsm"], writes=["tsm"])
                S.op(DVE, (lambda e, j=j: e.tensor_tensor(out=TSM, in0=TSM, in1=HINIT, op=ALU.subtract)), reads=["hinit", "tsm"], writes=["tsm"])
                S.op(DVE, (lambda e, j=j: e.scalar_tensor_tensor(out=HINIT, in0=TSM, scalar=V("msk", j), in1=HINIT, op0=ALU.mult, op1=ALU.add)),
                     reads=["tsm", "hinit", "vec"], writes=["hinit"])
            woutv = wout_d[l2].rearrange("(k p) d -> p k d", p=128)
            reset_run(HINIT)
            S.op(DVE, lambda e: e.memset(sm[:, 504:505], 0.0), writes=xkeys)
            ring_release(); ring_release()
            tl_box[0] = Rot("tl", tiles)

            def load_w2(hf):
                wingk, sl1 = ring_piece([(lambda s: s[:, 0:5120].rearrange("p (k f) -> p k f", k=8), winv[:, :, hf * 640:(hf + 1) * 640])])
                wing = sl1[:, 0:5120].rearrange("p (k f) -> p k f", k=8)
                woutk, sl2 = ring_piece([(lambda s: s[:, 0:5120].rearrange("p (k d) -> p k d", k=5), woutv[:, hf * 5:(hf + 1) * 5, :])])
                wouts = sl2[:, 0:5120].rearrange("p (k d) -> p k d", k=5)
                return (wing, wingk, wouts, woutk)
            run_pass(2, load_w2)
            S.op(DVE, lambda e: e.memset(sm[:, 503:504], 0.0), writes=[sk_] + tlkeys)
            ring_release()

        for l in layers:
            if "1" in phases:
                fence()
                ffn(l, f1g, f1u, f1d, "ffn1_norm")
            if "m" in phases:
                fence()
                if l % 2 == 0:
                    lru_layer(l)
                else:
                    pool_layer(l)
            if "2" in phases:
                fence()
                ffn(l, f2g, f2u, f2d, "ffn2_norm", mid_hook=(lambda l=l: ple_prefetch(l)) if "p" in phases else None)
            if "p" in phases:
                fence()
                if l not in ple_bufs:
                    ple_prefetch(l)
                ple(l)
        fence()

        ov = out_d.rearrange("(c p) t -> p c t", p=128)
        OUTS = [BIGreg[:, i * 4096:(i + 1) * 4096].rearrange("p (c t) -> p c t", c=DC) for i in range(2)]
        for tt in range(NT):
            ob = OUTS[tt % 2]
            if final:
                norm("final_norm", 0, lambda c, tt_, ob=ob: ob[:, c, :], lambda c, tt_: ("outs", tt_ % 2), tts=(tt,))
            else:
                for c in range(DC):
                    S.op(POOL, (lambda e, ob=ob, c=c, tt=tt: e.tensor_copy(ob[:, c, :], Hv[:, c, tsl(tt)])), reads=[hkeys(c, tt)], writes=[("outs", tt % 2)])
            S.dma(SP, (lambda e, ob=ob, tt=tt: e.dma_start(out=ov[:, :, tsl(tt)], in_=ob)), osem[tt % 2], reads=[("outs", tt % 2)],
                  writes=[("outd", tt), ("outs", tt % 2)])
        S.barrier(SP, [("outd", tt) for tt in range(NT)])
        S.emit(block, esem)
        build.nops = len(S.ops)
    return nc


VEC_SPECS.append(("eps", 1))
VOFF["eps"] = NV
NV += 1


def _prep_inputs(inp):
    x = np.asarray(inp["x"], np.float32)
    p = np.asarray(inp["p"], np.float32)
    gt = np.zeros((2, 2, 128, 2 * NGT, 128), np.float32)
    for l2 in range(2):
        for gi, nm in enumerate(("lru_w_a", "lru_w_x")):
            w = np.asarray(inp[nm], np.float32)[l2]
            dense = np.zeros((DR, DR), np.float32)
            for h in range(16):
                dense[h * 80:(h + 1) * 80, h * 80:(h + 1) * 80] = w[h]
            for hf in range(2):
                for (j, i), n in GT_IDX.items():
                    gt[l2, hf, :, gi * NGT + n, :] = dense[(hf * 5 + i) * 128:(hf * 5 + i + 1) * 128, (hf * 5 + j) * 128:(hf * 5 + j + 1) * 128]
    common_vec = {}
    for nm in ("ffn1_norm", "mix_norm", "ffn2_norm", "ple_norm", "final_norm", "pool_b", "pool_scale",
               "lru_conv_w", "lru_conv_b", "lru_b_a", "lru_b_x", "lru_a_param"):
        common_vec[nm] = _colpack(np.asarray(inp[nm], np.float32))
    shared = {k: np.ascontiguousarray(np.asarray(inp[k], np.float32)) for k in
              ("ffn1_w_gate", "ffn1_w_up", "ffn1_w_down", "ffn2_w_gate", "ffn2_w_up", "ffn2_w_down",
               "lru_w_in", "lru_w_out", "pool_w", "ple_w_gate", "ple_w_proj")}
    shared["lru_gt"] = gt
    in_maps = []
    for c in range(NCORE):
        b, k = divmod(c, 4)
        vecs = np.zeros((128, NV), np.float32)
        for nm, arr in common_vec.items():
            vecs[:, VOFF[nm]:VOFF[nm] + arr.shape[1]] = arr
        sel = np.zeros(4, np.float32)
        if k > 0:
            sel[k - 1] = 1.0
        msk = np.zeros(4, np.float32)
        msk[:k] = 1.0
        vecs[:, VOFF["sel"]:VOFF["sel"] + 4] = sel[None, :]
        vecs[:, VOFF["msk"]:VOFF["msk"] + 4] = msk[None, :]
        ic = np.zeros((4, 16), np.float32)
        for g in range(4):
            win = 2 << g
            for t in range(16):
                ic[g, t] = np.float32(1.0) / np.float32(min(t + 1, win) if k == 0 else win)
        vecs[:, VOFF["invcnt"]:VOFF["invcnt"] + 64] = ic.reshape(1, 64)
        vecs[:, VOFF["eps"]] = EPS
        m = dict(shared)
        m["xT"] = np.ascontiguousarray(x[b, k * T:(k + 1) * T, :].T)
        m["pT"] = np.ascontiguousarray(p[:, b, k * T:(k + 1) * T, :].transpose(0, 2, 1))
        m["vecs"] = vecs
        in_maps.append(m)
    return in_maps


_NC_CACHE = {}


def kernel(**inputs):
    in_maps = _prep_inputs(inputs)
    if "nc" not in _NC_CACHE:
        _NC_CACHE["nc"] = build()
    nc = _NC_CACHE["nc"]
    res = run_bass_kernel_spmd(nc, in_maps, core_ids=list(range(NCORE)))
    out = np.empty((2, 4 * T, D), np.float32)
    for c in range(NCORE):
        b, k = divmod(c, 4)
        out[b, k * T:(k + 1) * T, :] = res.results[c]["outT"].T
    return out
```

```python
import numpy as np
from contextlib import ExitStack
import concourse.bass as bass
import concourse.mybir as mybir
from concourse.bass_utils import run_bass_kernel_spmd

F32 = mybir.dt.float32
BF16 = mybir.dt.bfloat16
AF = mybir.ActivationFunctionType
ALU = mybir.AluOpType

PE, ACT, DVE, POOL, SP = "pe", "act", "dve", "pool", "sp"
ENGS = (PE, ACT, DVE, POOL, SP)

NCORE = 8
INTERLEAVE = False
D = 1024
DC = 8
T = 2048
TT = 512
NT = T // TT
FF = 2816
FC = 22
DR = 1280
RC = 10
DEPTH = 4
EPS = 1e-6
NBRS = {0: [0, 1], 1: [0, 1, 2], 2: [1, 2, 3], 3: [2, 3, 4], 4: [3, 4]}
GT_IDX = {}
_n = 0
for _j in range(5):
    for _i in NBRS[_j]:
        GT_IDX[(_j, _i)] = _n
        _n += 1
NGT = _n


class Op:
    __slots__ = ("eng", "fn", "deps", "is_dma", "sem", "target", "seq", "idx")


class Sched:
    def __init__(self, nc):
        self.nc = nc
        self.ops = []
        self.last_w = {}
        self.readers = {}
        self.cnt = {e: 0 for e in ENGS}
        self.dma_sem_total = {}
        self.phase_op = None
        self.last_eng_op = {}
        self.dmas_since = []

    def fence(self, fn):
        o = Op()
        o.eng = DVE
        o.fn = fn
        o.idx = len(self.ops)
        o.deps = set(self.last_eng_op.values()) | set(self.dmas_since)
        o.is_dma = False
        self.cnt[DVE] += 1
        o.seq = self.cnt[DVE]
        o.sem = None
        o.target = None
        self.ops.append(o)
        self.phase_op = o.idx
        self.last_eng_op[DVE] = o.idx
        self.dmas_since = []
        return o

    def _mk(self, eng, fn, reads, writes, nophase=False):
        o = Op()
        o.eng = eng
        o.fn = fn
        o.idx = len(self.ops)
        deps = set()
        if self.phase_op is not None and not nophase:
            deps.add(self.phase_op)
        for k in reads:
            w = self.last_w.get(k)
            if w is not None:
                deps.add(w)
        for k in writes:
            w = self.last_w.get(k)
            if w is not None:
                deps.add(w)
            for r in self.readers.get(k, ()):
                deps.add(r)
        deps.discard(o.idx)
        o.deps = deps
        for k in reads:
            self.readers.setdefault(k, []).append(o.idx)
        for k in writes:
            self.last_w[k] = o.idx
            self.readers[k] = []
        self.ops.append(o)
        return o

    def op(self, eng, fn, reads=(), writes=()):
        o = self._mk(eng, fn, reads, writes)
        o.is_dma = False
        self.cnt[eng] += 1
        o.seq = self.cnt[eng]
        o.sem = None
        o.target = None
        self.last_eng_op[eng] = o.idx
        return o

    def barrier(self, eng, reads):
        o = self._mk(eng, None, reads, ())
        o.is_dma = False
        o.seq = self.cnt[eng]
        o.sem = None
        o.target = None
        return o

    def dma(self, eng, fn, sem, reads=(), writes=(), inc=16, nophase=False):
        o = self._mk(eng, fn, reads, writes, nophase=nophase)
        self.dmas_since.append(o.idx)
        o.is_dma = True
        o.sem = sem
        key = id(sem)
        self.dma_sem_total[key] = self.dma_sem_total.get(key, 0) + inc
        o.target = self.dma_sem_total[key]
        o.seq = inc
        return o

    def emit(self, block, esem):
        ops = self.ops
        per = {e: [o for o in ops if o.eng == e] for e in ENGS}

        def run(eng_name, engine):
            waited = {}
            for o in per[eng_name]:
                need = {}
                for d in o.deps:
                    p = ops[d]
                    if p.is_dma:
                        s, v = p.sem, p.target
                    else:
                        if p.eng == eng_name:
                            if eng_name in (PE, SP):
                                continue
                        s, v = esem[p.eng], p.seq
                    k = id(s)
                    if need.get(k, (None, 0))[1] < v:
                        need[k] = (s, v)
                for k, (s, v) in need.items():
                    if waited.get(k, 0) >= v:
                        continue
                    engine.wait_ge(s, v)
                    waited[k] = v
                if o.fn is None:
                    continue
                ins = o.fn(engine)
                if o.is_dma:
                    ins.then_inc(o.sem, o.seq)
                else:
                    ins.then_inc(esem[eng_name], 1)

        @block.tensor
        def _(e):
            run(PE, e)

        @block.scalar
        def _(e):
            run(ACT, e)

        @block.vector
        def _(e):
            run(DVE, e)

        @block.gpsimd
        def _(e):
            run(POOL, e)

        @block.sync
        def _(e):
            run(SP, e)


VEC_SPECS = [
    ("ffn1_norm", DEPTH * DC), ("mix_norm", DEPTH * DC), ("ffn2_norm", DEPTH * DC), ("ple_norm", DEPTH * DC),
    ("final_norm", DC), ("pool_b", 2 * DC), ("pool_scale", 2 * DC),
    ("lru_conv_w", 2 * 4 * RC), ("lru_conv_b", 2 * RC), ("lru_b_a", 2 * RC), ("lru_b_x", 2 * RC),
    ("lru_a_param", 2 * RC),
    ("sel", 4), ("msk", 4), ("invcnt", 4 * 16),
]
VOFF = {}
_o = 0
for _nm, _n2 in VEC_SPECS:
    VOFF[_nm] = _o
    _o += _n2
NV = _o


def _colpack(v):
    v = np.asarray(v, np.float32)
    C = v.shape[-1] // 128
    lead = int(np.prod(v.shape[:-1])) if v.ndim > 1 else 1
    return np.ascontiguousarray(v.reshape(lead, C, 128).transpose(2, 0, 1).reshape(128, lead * C))


def build(layers=(0, 1, 2, 3), final=True, stop_after=None, phases="1m2p"):
    nc = bass.Bass("TRN2", target_bir_lowering=False)
    dt_in = lambda name, shape: nc.dram_tensor(name, shape, F32, kind="ExternalInput").ap()
    xT_d = dt_in("xT", [D, T])
    pT_d = dt_in("pT", [DEPTH, 256, T])
    vec_d = dt_in("vecs", [128, NV])
    f1g = dt_in("ffn1_w_gate", [DEPTH, D, FF]); f1u = dt_in("ffn1_w_up", [DEPTH, D, FF]); f1d = dt_in("ffn1_w_down", [DEPTH, FF, D])
    f2g = dt_in("ffn2_w_gate", [DEPTH, D, FF]); f2u = dt_in("ffn2_w_up", [DEPTH, D, FF]); f2d = dt_in("ffn2_w_down", [DEPTH, FF, D])
    win_d = dt_in("lru_w_in", [2, D, 2 * DR])
    wout_d = dt_in("lru_w_out", [2, DR, D])
    gt_d = dt_in("lru_gt", [2, 2, 128, 2 * NGT, 128])
    poolw_d = dt_in("pool_w", [2, 4, 256, 256])
    pleg_d = dt_in("ple_w_gate", [DEPTH, D, D])
    plep_d = dt_in("ple_w_proj", [DEPTH, 256, D])
    out_d = nc.dram_tensor("outT", [D, T], F32, kind="ExternalOutput").ap()
    ex_widths = []
    for l_ in layers:
        ex_widths += [128, 128] if l_ % 2 == 0 else [128]
    ex_bufs = [(nc.dram_tensor(f"ex_in{i}", [128, w], F32).ap(), nc.dram_tensor(f"ex_out{i}", [4 * 128, w], F32).ap())
               for i, w in enumerate(ex_widths)]

    with ExitStack() as es:
        H_W = DC * T
        HN_W = DC * T // 2
        BIG_W = 12 * T // 2
        SLOT_W = 2560
        NSLOT = 3
        TMP_W = 7424
        arena = es.enter_context(nc.sbuf_tensor("arena", [128, H_W + HN_W + BIG_W + NSLOT * SLOT_W + TMP_W], F32))
        vec = es.enter_context(nc.sbuf_tensor("vec", [128, NV], F32))
        sm = es.enter_context(nc.sbuf_tensor("sm", [128, 512], F32))
        onesb = es.enter_context(nc.sbuf_tensor("onesb", [128, 128], BF16))
        o0 = 0
        Hreg = arena[:, o0:o0 + H_W]; o0 += H_W
        HNreg = arena[:, o0:o0 + HN_W]; hb0 = o0; o0 += HN_W
        BIGreg = arena[:, o0:o0 + BIG_W]; big0 = o0; o0 += BIG_W
        SLOTreg = [arena[:, o0 + i * SLOT_W: o0 + (i + 1) * SLOT_W] for i in range(NSLOT)]; o0 += NSLOT * SLOT_W
        tmp0 = o0
        Hv = Hreg.rearrange("p (c t) -> p c t", c=DC)
        HNb = HNreg.bitcast(BF16).rearrange("p (c t) -> p c t", c=DC)
        HIDb = BIGreg.bitcast(BF16).rearrange("p (c t) -> p c t", c=12)
        slotb = [s.bitcast(BF16) for s in SLOTreg]

        def tmp_f32(off, n):
            return arena[:, tmp0 + off: tmp0 + off + n]

        def big_f32(off, n):
            return arena[:, big0 + off: big0 + off + n]

        ps = [es.enter_context(nc.psum_tensor(f"ps{i}", [128, TT], F32)) for i in range(8)]
        esem = {e: es.enter_context(nc.semaphore(f"s_{e}")) for e in ENGS}
        ssem = [es.enter_context(nc.semaphore(f"slot{i}")) for i in range(NSLOT)]
        xsem = [es.enter_context(nc.semaphore(f"xs{i}")) for i in range(DC)]
        osem = [es.enter_context(nc.semaphore(f"os{i}")) for i in range(2)]
        vsem = es.enter_context(nc.semaphore("vs"))
        psem = es.enter_context(nc.semaphore("psm"))
        asem = [es.enter_context(nc.semaphore(f"as{i}")) for i in range(2)]
        exsem = [es.enter_context(nc.semaphore(f"ex{i}")) for i in range(3)]
        block = es.enter_context(nc.Block())
        S = Sched(nc)

        V = lambda name, i=0: vec[:, VOFF[name] + i: VOFF[name] + i + 1]

        def tsl(tt):
            return slice(tt * TT, (tt + 1) * TT)

        class Rot:
            def __init__(self, name, aps):
                self.name = name; self.aps = aps; self.i = 0

            def next(self):
                k = self.i % len(self.aps); self.i += 1
                return (self.name, k), self.aps[k]

        psrot = {}

        def psbank(role, banks):
            if role not in psrot:
                psrot[role] = [0, banks]
            r = psrot[role]
            b = r[1][r[0] % len(r[1])]; r[0] += 1
            return ("ps", b), ps[b]

        ring_order = list(range(NSLOT))
        ring_held = []

        def ring_piece(dmas):
            s = ring_order.pop(0); ring_order.append(s)
            key = ("slot", s)
            for dst_fn, src in dmas:
                S.dma(POOL, (lambda e, dst_fn=dst_fn, src=src, s=s: e.dma_start(out=dst_fn(slotb[s]), in_=src)),
                      ssem[s], writes=[key], nophase=True)
            return key, slotb[s]

        def ring_reserve():
            s = ring_order.pop(0)
            ring_held.append(s)
            return ("slot", s), SLOTreg[s]

        def ring_release():
            ring_order.append(ring_held.pop())

        S.dma(SP, lambda e: e.dma_start(out=vec[:], in_=vec_d), vsem, writes=["vec"])
        S.op(POOL, lambda e: e.memset(onesb[:], 1.0), writes=["ones"])
        xv = xT_d.rearrange("(c p) t -> p c t", p=128)
        for c in range(DC):
            S.dma(SP, (lambda e, c=c: e.dma_start(out=Hv[:, c, :], in_=xv[:, c, :])), xsem[c],
                  writes=[("h", c, tt) for tt in range(NT)])

        CNEG = lambda l2, c: sm[:, l2 * RC + c: l2 * RC + c + 1]
        CH = lambda l2, c: sm[:, 20 + l2 * RC + c: 20 + l2 * RC + c + 1]
        PBS = lambda l2, c: sm[:, 40 + l2 * DC + c: 40 + l2 * DC + c + 1]
        HBA = lambda l2, c: sm[:, 56 + l2 * RC + c: 56 + l2 * RC + c + 1]
        HBX = lambda l2, c: sm[:, 76 + l2 * RC + c: 76 + l2 * RC + c + 1]
        ap_ = vec[:, VOFF["lru_a_param"]: VOFF["lru_a_param"] + 2 * RC]
        S.op(ACT, lambda e: e.activation(out=sm[:, 0:20], in_=ap_, func=AF.Exp, scale=-1.0), reads=["vec"], writes=["sm_c"])
        S.op(ACT, lambda e: e.activation(out=sm[:, 0:20], in_=sm[:, 0:20], func=AF.Ln, bias=1.0), reads=["sm_c"], writes=["sm_c"])
        S.op(DVE, lambda e: e.tensor_scalar(out=sm[:, 20:40], in0=sm[:, 0:20], scalar1=-4.0, scalar2=None, op0=ALU.mult), reads=["sm_c"], writes=["sm_c2"])
        S.op(DVE, lambda e: e.tensor_scalar(out=sm[:, 0:20], in0=sm[:, 0:20], scalar1=-8.0, scalar2=None, op0=ALU.mult), reads=["sm_c", "sm_c2"], writes=["sm_c"])
        S.op(DVE, lambda e: e.tensor_tensor(out=sm[:, 40:56], in0=vec[:, VOFF["pool_b"]:VOFF["pool_b"] + 16],
                                            in1=vec[:, VOFF["pool_scale"]:VOFF["pool_scale"] + 16], op=ALU.mult), reads=["vec"], writes=["sm_pbs"])
        S.op(DVE, lambda e: e.tensor_scalar(out=sm[:, 56:76], in0=vec[:, VOFF["lru_b_a"]:VOFF["lru_b_a"] + 20], scalar1=0.5, scalar2=None, op0=ALU.mult), reads=["vec"], writes=["sm_hb"])
        S.op(DVE, lambda e: e.tensor_scalar(out=sm[:, 76:96], in0=vec[:, VOFF["lru_b_x"]:VOFF["lru_b_x"] + 20], scalar1=0.5, scalar2=None, op0=ALU.mult), reads=["vec", "sm_hb"], writes=["sm_hb"])

        hkeys = lambda c, tt: ("h", c, tt)

        def fence():
            S.fence(lambda e: e.memset(sm[:, 500:501], 0.0))

        sq_rot = Rot("sq", [tmp_f32(i * 256, 256).bitcast(BF16) for i in range(2)])
        rs_rot = Rot("rs", [tmp_f32(512 + i * 512, 512) for i in range(2)])
        TMPB = 512 + 1024

        def norm(gname, gidx, dst_fn, dst_key_fn, tts=range(NT)):
            for tt in tts:
                pk, pb = psbank("norm", [6, 7])
                for c in range(DC):
                    sk, sq = sq_rot.next()
                    S.op(ACT, (lambda e, sq=sq, c=c, tt=tt: e.activation(out=sq, in_=Hv[:, c, tsl(tt)], func=AF.Square)),
                         reads=[hkeys(c, tt)], writes=[sk])
                    S.op(PE, (lambda e, sq=sq, pb=pb, c=c: e.matmul(pb[:], lhsT=onesb[:], rhs=sq, start=(c == 0), stop=(c == DC - 1))),
                         reads=[sk, "ones"], writes=[pk])
                rk, rs = rs_rot.next()
                S.op(ACT, (lambda e, rs=rs, pb=pb: e.activation(out=rs, in_=pb[:], func=AF.Ln, scale=1.0 / D, bias=V("eps"))),
                     reads=[pk, "vec"], writes=[rk])
                S.op(ACT, (lambda e, rs=rs: e.activation(out=rs, in_=rs, func=AF.Exp, scale=-0.5)), reads=[rk], writes=[rk])
                for c in range(DC):
                    S.op(DVE, (lambda e, rs=rs, c=c, tt=tt: e.scalar_tensor_tensor(
                        out=dst_fn(c, tt), in0=Hv[:, c, tsl(tt)], scalar=V(gname, gidx * DC + c), in1=rs, op0=ALU.mult, op1=ALU.mult)),
                         reads=[hkeys(c, tt), rk, "vec"], writes=[dst_key_fn(c, tt)])

        hn_dst = lambda c, tt: HNb[:, c, tsl(tt)]
        hn_key = lambda c, tt: ("hn", c, tt)

        sg_rot = Rot("sg", [tmp_f32(TMPB + i * 512, 512) for i in range(3)])
        FFN_GROUPS = [list(range(0, 12)), list(range(12, 22))]

        def ffn(l, wg, wu, wd, nname, mid_hook=None):
            normed = set()

            def need(tt):
                for t_ in (tt, tt + 1):
                    if t_ < NT and t_ not in normed:
                        normed.add(t_)
                        norm(nname, l, hn_dst, hn_key, tts=(t_,))
            wgv = wg[l].rearrange("(k p) f -> p k f", p=128)
            wuv = wu[l].rearrange("(k p) f -> p k f", p=128)
            for gi_, grp in enumerate(FFN_GROUPS):
                if gi_ == 1 and mid_hook is not None:
                    mid_hook()
                for pi in range(0, len(grp), 2):
                    f0 = grp[pi]
                    key, sl = ring_piece([
                        (lambda s: s[:, 0:2048].rearrange("p (k f) -> p k f", k=8), wgv[:, :, f0 * 128:(f0 + 2) * 128]),
                        (lambda s: s[:, 2048:4096].rearrange("p (k f) -> p k f", k=8), wuv[:, :, f0 * 128:(f0 + 2) * 128]),
                    ])
                    wgs = sl[:, 0:2048].rearrange("p (k f) -> p k f", k=8)
                    wus = sl[:, 2048:4096].rearrange("p (k f) -> p k f", k=8)
                    for fo in range(2):
                        fi = pi + fo
                        for tt in range(NT):
                            need(tt)
                            gk, gb = psbank("ffg", [0, 1])
                            uk, ub = psbank("ffu", [2, 3])
                            for k in range(DC):
                                S.op(PE, (lambda e, gb=gb, k=k, fo=fo, tt=tt, wgs=wgs: e.matmul(
                                    gb[:], lhsT=wgs[:, k, fo * 128:(fo + 1) * 128], rhs=HNb[:, k, tsl(tt)], start=(k == 0), stop=(k == DC - 1))),
                                     reads=[key, hn_key(k, tt)], writes=[gk])
                            for k in range(DC):
                                S.op(PE, (lambda e, ub=ub, k=k, fo=fo, tt=tt, wus=wus: e.matmul(
                                    ub[:], lhsT=wus[:, k, fo * 128:(fo + 1) * 128], rhs=HNb[:, k, tsl(tt)], start=(k == 0), stop=(k == DC - 1))),
                                     reads=[key, hn_key(k, tt)], writes=[uk])
                            sk, sg = sg_rot.next()
                            S.op(ACT, (lambda e, sg=sg, gb=gb: e.activation(out=sg, in_=gb[:], func=AF.Silu)), reads=[gk], writes=[sk])
                            S.op(DVE, (lambda e, sg=sg, ub=ub, fi=fi, tt=tt: e.tensor_tensor(out=HIDb[:, fi, tsl(tt)], in0=sg, in1=ub[:], op=ALU.mult)),
                                 reads=[sk, uk], writes=[("hid", fi, tt)])
                nf = len(grp)
                wdv = wd[l][grp[0] * 128:(grp[0] + nf) * 128, :].rearrange("(k p) d -> p k d", p=128)
                for jp in range(0, DC, 2):
                    key, sl = ring_piece([(lambda s, nf=nf: s[:, 0:nf * 256].rearrange("p (k d) -> p k d", k=nf), wdv[:, :, jp * 128:(jp + 2) * 128])])
                    wds = sl[:, 0:nf * 256].rearrange("p (k d) -> p k d", k=nf)
                    for jo in range(2):
                        j = jp + jo
                        for tt in range(NT):
                            dk, db = psbank("ffd", [4, 5])
                            for fi in range(nf):
                                S.op(PE, (lambda e, db=db, fi=fi, jo=jo, tt=tt, wds=wds, nf=nf: e.matmul(
                                    db[:], lhsT=wds[:, fi, jo * 128:(jo + 1) * 128], rhs=HIDb[:, fi, tsl(tt)], start=(fi == 0), stop=(fi == nf - 1))),
                                     reads=[key, ("hid", fi, tt)], writes=[dk])
                            S.op(DVE, (lambda e, db=db, j=j, tt=tt: e.scalar_tensor_tensor(
                                out=Hv[:, j, tsl(tt)], in0=db[:], scalar=0.5, in1=Hv[:, j, tsl(tt)], op0=ALU.mult, op1=ALU.add)),
                                 reads=[dk, hkeys(j, tt)], writes=[hkeys(j, tt)])

        ple_bufs = {}

        def ple_prefetch(l):
            PTb = tmp_f32(3072, 2048).bitcast(BF16).rearrange("p (c t) -> p c t", c=2)
            wps = tmp_f32(5120, 1024).bitcast(BF16).rearrange("p (k d) -> p k d", k=2)
            S.dma(POOL, lambda e: e.dma_start(out=PTb, in_=pT_d[l].rearrange("(c p) t -> p c t", p=128)), psem, writes=["pt"])
            S.dma(POOL, lambda e: e.dma_start(out=wps, in_=plep_d[l].rearrange("(k p) d -> p k d", p=128)), asem[0], writes=["wproj"])
            ple_bufs[l] = (PTb, wps, "wproj")

        def ple(l):
            normed = set()

            def need(tt):
                for t_ in (tt, tt + 1):
                    if t_ < NT and t_ not in normed:
                        normed.add(t_)
                        norm("ple_norm", l, hn_dst, hn_key, tts=(t_,))
            PTb, wps, kp = ple_bufs[l]
            wgv = pleg_d[l].rearrange("(k p) d -> p k d", p=128)
            pg_rot = Rot("pg", [tmp_f32(TMPB + i * 512, 512) for i in range(3)])
            for jp in range(0, DC, 2):
                key, sl = ring_piece([(lambda s: s[:, 0:2048].rearrange("p (k d) -> p k d", k=8), wgv[:, :, jp * 128:(jp + 2) * 128])])
                wgs = sl[:, 0:2048].rearrange("p (k d) -> p k d", k=8)
                for jo in range(2):
                    j = jp + jo
                    for tt in range(NT):
                        need(tt)
                        ak, ab = psbank("ffg", [0, 1])
                        bk, bb = psbank("ffu", [2, 3])
                        for k in range(DC):
                            S.op(PE, (lambda e, ab=ab, k=k, jo=jo, tt=tt, wgs=wgs: e.matmul(
                                ab[:], lhsT=wgs[:, k, jo * 128:(jo + 1) * 128], rhs=HNb[:, k, tsl(tt)], start=(k == 0), stop=(k == DC - 1))),
                                 reads=[key, hn_key(k, tt)], writes=[ak])
                        for k in range(2):
                            S.op(PE, (lambda e, bb=bb, k=k, j=j, tt=tt: e.matmul(
                                bb[:], lhsT=wps[:, k, j * 128:(j + 1) * 128], rhs=PTb[:, k, tsl(tt)], start=(k == 0), stop=(k == 1))),
                                 reads=[kp, "pt"], writes=[bk])
                        gk, g = pg_rot.next()
                        S.op(ACT, (lambda e, g=g, ab=ab: e.activation(out=g, in_=ab[:], func=AF.Sigmoid)), reads=[ak], writes=[gk])
                        S.op(DVE, (lambda e, g=g, bb=bb: e.tensor_tensor(out=g, in0=g, in1=bb[:], op=ALU.mult)), reads=[gk, bk], writes=[gk])
                        S.op(POOL, (lambda e, g=g, j=j, tt=tt: e.tensor_tensor(out=Hv[:, j, tsl(tt)], in0=Hv[:, j, tsl(tt)], in1=g, op=ALU.add)),
                             reads=[gk, hkeys(j, tt)], writes=[hkeys(j, tt)])

        ex_cnt = {"i": 0}

        def exchange(src_ap, width, src_keys, dst_ap, dst_key):
            i = ex_cnt["i"]; ex_cnt["i"] += 1
            exi, exo = ex_bufs[i]
            assert exi.shape[1] == width, (exi.shape, width)
            S.dma(POOL, lambda e: e.dma_start(out=exi, in_=src_ap), exsem[0], reads=src_keys, writes=[("exin", i)])
            S.dma(POOL, lambda e: e.collective_compute("AllGather", ALU.bypass, replica_groups=[[0, 1, 2, 3], [4, 5, 6, 7]],
                                                       ins=[exi], outs=[exo]), exsem[1], reads=[("exin", i)], writes=[("exout", i)], inc=1)
            S.dma(POOL, lambda e: e.dma_start(out=dst_ap, in_=exo.rearrange("(r p) f -> p r f", p=128)), exsem[2],
                  reads=[("exout", i)], writes=[dst_key])

        def select_prev(dst_ap, g_ap, width, dst_keys, g_key, eng=DVE):
            S.op(DVE, lambda e: e.tensor_scalar(out=dst_ap, in0=g_ap[:, 0, :], scalar1=V("sel", 0), scalar2=None, op0=ALU.mult),
                 reads=[g_key, "vec"], writes=dst_keys)
            for j in range(1, 4):
                S.op(DVE, (lambda e, j=j: e.scalar_tensor_tensor(out=dst_ap, in0=g_ap[:, j, :], scalar=V("sel", j), in1=dst_ap, op0=ALU.mult, op1=ALU.add)),
                     reads=[g_key, "vec"] + dst_keys, writes=dst_keys)

        def pool_layer(l):
            l2 = l // 2
            HW = 16 + T
            HN32 = arena[:, hb0: hb0 + DC * HW].rearrange("p (c t) -> p c t", c=DC)
            extra0 = hb0 + DC * HW - big0
            EXS = big_f32(extra0, 128)
            EXG = big_f32(extra0 + 128, 512).rearrange("p (r f) -> p r f", r=4)
            S.op(POOL, lambda e: e.memset(EXS, 0.0), writes=["pexs"])
            hn32_key = lambda c, tt: ("hn32", c, tt)
            norm("mix_norm", l, lambda c, tt: HN32[:, c, 16 + tt * TT: 16 + (tt + 1) * TT], hn32_key, tts=(3, 0, 1, 2))
            for c in range(DC):
                S.op(POOL, (lambda e, c=c: e.tensor_copy(EXS[:, c * 15:(c + 1) * 15], HN32[:, c, 16 + T - 15: 16 + T])),
                     reads=[hn32_key(c, 3)], writes=["pexs"])
            exchange(EXS, 128, ["pexs"], EXG, "pexg")
            HALO = big_f32(extra0 + 640, 120)
            select_prev(HALO, EXG[:, :, 0:120], 120, ["phalo"], "pexg")
            for c in range(DC):
                S.op(POOL, (lambda e, c=c: e.tensor_copy(HN32[:, c, 1:16], HALO[:, c * 15:(c + 1) * 15])), reads=["phalo"], writes=[("hn32h", c)])
            kw, slw = ring_piece([(lambda s: s[:, 0:2048].rearrange("p (g k d) -> p g k d", g=4, k=2),
                                   poolw_d[l2].rearrange("g (k p) d -> p g k d", p=128))])
            pw = slw[:, 0:2048].rearrange("p (g k d) -> p g k d", g=4, k=2)
            WB = 528
            wk_rot = Rot("pwk", [tmp_f32(TMPB + i * WB, WB) for i in range(3)])
            U_rot = [tmp_f32(TMPB + 3 * WB + i * 2048, 2048).bitcast(BF16).rearrange("p (c t) -> p c t", c=DC) for i in range(2)]
            assert TMPB + 3 * WB + 2 * 2048 <= TMP_W
            pt_rot = Rot("ptmp", [big_f32(extra0 + 1024 + i * 512, 512) for i in range(3)])
            wcnt = [0]
            for tt in range(NT):
                Ub = U_rot[tt % 2]
                ukey = lambda c: ("pu", tt % 2, c)
                for c in range(DC):
                    g = c // 2
                    win = 2 << g
                    base = tt * TT
                    src = HN32[:, c, base: base + WB]
                    rdk = [hn32_key(c, tt), ("hn32h", c)] + ([hn32_key(c, tt - 1)] if tt > 0 else [])
                    cur = src; curk = rdk; sh = 1
                    while sh < win:
                        wkk, wk = wk_rot.next()
                        lo = 2 * sh
                        wcnt[0] += 1
                        S.op(POOL if wcnt[0] % 3 == 2 else DVE,
                             (lambda e, wk=wk, cur=cur, sh=sh, lo=lo: e.tensor_tensor(out=wk[:, lo:WB], in0=cur[:, lo:WB], in1=cur[:, lo - sh:WB - sh], op=ALU.add)),
                             reads=curk, writes=[wkk])
                        cur = wk; curk = [wkk]; sh *= 2
                    S.op(DVE, (lambda e, cur=cur, c=c, win=win, src=src, Ub=Ub: e.scalar_tensor_tensor(
                        out=Ub[:, c, :], in0=cur[:, 16:WB], scalar=1.0 / win, in1=src[:, 16:WB], op0=ALU.mult, op1=ALU.subtract)),
                         reads=curk + rdk, writes=[ukey(c)])
                    if tt == 0:
                        wkk2, wk2 = wk_rot.next()
                        S.op(DVE, (lambda e, cur=cur, wk2=wk2, g=g: e.tensor_tensor(out=wk2[:, 0:16], in0=cur[:, 16:32],
                                                                                     in1=vec[:, VOFF["invcnt"] + g * 16: VOFF["invcnt"] + (g + 1) * 16], op=ALU.mult)),
                             reads=curk + ["vec"], writes=[wkk2])
                        S.op(DVE, (lambda e, wk2=wk2, c=c, src=src, Ub=Ub: e.tensor_tensor(out=Ub[:, c, 0:16], in0=wk2[:, 0:16], in1=src[:, 16:32], op=ALU.subtract)),
                             reads=[wkk2] + rdk, writes=[ukey(c)])
                for g in range(4):
                    for jo in range(2):
                        j = 2 * g + jo
                        pk_, pb_ = psbank("ffd", [4, 5])
                        for k in range(2):
                            S.op(PE, (lambda e, pb_=pb_, g=g, k=k, jo=jo, Ub=Ub: e.matmul(
                                pb_[:], lhsT=pw[:, g, k, jo * 128:(jo + 1) * 128], rhs=Ub[:, 2 * g + k, :], start=(k == 0), stop=(k == 1))),
                                 reads=[kw, ukey(2 * g + k)], writes=[pk_])
                        tk, tp = pt_rot.next()
                        S.op(ACT, (lambda e, tp=tp, pb_=pb_, j=j: e.activation(out=tp, in_=pb_[:], func=AF.Identity,
                                                                               scale=V("pool_scale", l2 * DC + j), bias=PBS(l2, j))),
                             reads=[pk_, "vec", "sm_pbs"], writes=[tk])
                        S.op(POOL, (lambda e, tp=tp, j=j, tt=tt: e.tensor_tensor(out=Hv[:, j, tsl(tt)], in0=Hv[:, j, tsl(tt)], in1=tp, op=ALU.add)),
                             reads=[tk, hkeys(j, tt)], writes=[hkeys(j, tt)])
            allk = [hn32_key(c, tt) for c in range(DC) for tt in range(NT)] + [("hn32h", c) for c in range(DC)] + ["pexs", "pexg", "phalo"]
            return allk

        def lru_layer(l):
            l2 = l // 2
            norm("mix_norm", l, hn_dst, hn_key)
            winx = [BIGreg[:, hf * 2560:(hf + 1) * 2560].bitcast(BF16).rearrange("p (k f) -> p k f", k=8) for hf in range(2)]
            GT = BIGreg[:, 5120:5120 + 1664].bitcast(BF16).rearrange("p (n f) -> p n f", n=2 * NGT)
            gts = [GT, GT]
            winv = win_d[l2].rearrange("(k p) f -> p k f", p=128)
            for hf in range(2):
                S.dma(POOL, (lambda e, hf=hf: e.dma_start(out=winx[hf], in_=winv[:, :, DR + hf * 640: DR + (hf + 1) * 640])), (asem[0], psem)[hf],
                      writes=[("A", hf)])

            def load_gt(hf):
                S.dma(POOL, (lambda e: e.dma_start(out=GT, in_=gt_d[l2, hf])), asem[1], writes=["gt"])
            b0 = 5120 + 1664
            XCb = [big_f32(b0 + i * 1280, 1280).bitcast(BF16).rearrange("p (c t) -> p c t", c=5) for i in range(2)]
            b0 += 2560 - 1280
            Yb = big_f32(b0 + 1280, 1280).bitcast(BF16).rearrange("p (c t) -> p c t", c=5)
            smallb = b0 + 2560
            HALO2 = [big_f32(smallb + i * 32, 30).rearrange("p (c t) -> p c t", c=RC) for i in range(2)]
            HALO0f = big_f32(smallb + 64, 30)
            STATE = big_f32(smallb + 96, RC)
            TSUM = big_f32(smallb + 112, RC * NT).rearrange("p (c t) -> p c t", c=RC)
            HC = [big_f32(smallb + 160 + i * 4, 3) for i in range(10)]
            HCT = [big_f32(smallb + 200 + i * 4, 3) for i in range(10)]
            HINIT = big_f32(smallb + 448, RC)
            TSM = big_f32(smallb + 464, RC)
            RS1 = big_f32(smallb + 480, RC)
            ykeys = [("y", ci) for ci in range(5)]
            EXS_ = big_f32(b0 + 1280, 128)
            EXG_ = big_f32(b0 + 1280 + 128, 512).rearrange("p (r f) -> p r f", r=4)
            EX1 = EXS_; EX1G = EXG_; EX2 = EXS_; EX2G = EXG_
            S.op(POOL, lambda e: e.memset(EXS_, 0.0), writes=["ex1", "ex2a", "ex2b"] + ykeys)
            XC2 = [tmp_f32(i * 2560, 2560).rearrange("p (c t) -> p c t", c=5) for i in range(2)]
            sk_, sslot = ring_reserve()
            tiles = [tmp_f32(5120 + i * 512, 512) for i in range((TMP_W - 5120) // 512)] + \
                    [big_f32(b0 + 3072 + i * 512, 512) for i in range((BIG_W - b0 - 3072) // 512)] + \
                    [sslot[:, i * 512:(i + 1) * 512] for i in range(SLOT_W // 512)]
            assert len(tiles) >= 10, len(tiles)
            extra = [ring_reserve(), ring_reserve()]
            tiles_p1 = tiles + [sl_[:, i * 512:(i + 1) * 512] for _, sl_ in extra for i in range(SLOT_W // 512)]
            tl_box = [Rot("tl", tiles_p1)]
            tlkeys = [("tl", i) for i in range(len(tiles))]
            xkeys = [k_ for k_, _ in extra] + [("tl", i) for i in range(len(tiles), len(tiles_p1))]

            pk3, pb3 = psbank("ffd", [4, 5])
            for c in range(RC):
                hf, ci = divmod(c, 5)
                for k in range(DC):
                    S.op(PE, (lambda e, c=c, hf=hf, ci=ci, k=k: e.matmul(pb3[:, c * 4: c * 4 + 3], lhsT=winx[hf][:, k, ci * 128:(ci + 1) * 128],
                                                                      rhs=HNb[:, k, T - 3:T], start=(k == 0), stop=(k == DC - 1))),
                         reads=[("A", hf), hn_key(k, 3)], writes=[pk3])
            S.op(ACT, lambda e: e.activation(out=EX1[:, 0:30].rearrange("p (c t) -> p c t", c=RC), in_=pb3[:, 0:40].rearrange("p (c t) -> p c t", c=RC)[:, :, 0:3], func=AF.Identity),
                 reads=[pk3], writes=["ex1"])
            exchange(EX1, 128, ["ex1"], EX1G, "ex1g")
            select_prev(HALO0f, EX1G[:, :, 0:30], 30, ["halo0"], "ex1g")
            S.op(DVE, lambda e: e.memset(sm[:, 502:503], 0.0),
                 writes=[("sq", 0), ("sq", 1), ("rs", 0), ("rs", 1), sk_] + [("xc", b_, ci) for b_ in range(2) for ci in range(5)] + tlkeys + xkeys)

            cw = lambda kk, c: V("lru_conv_w", l2 * 4 * RC + kk * RC + c)
            hc_i = [0]

            def stage1(u, hf, tt):
                wk = ("A", hf)
                XC = XC2[u % 2]; XCbb = XCb[u % 2]
                Hin = HALO2[tt % 2]; Hout = HALO2[(tt + 1) % 2]
                for ci in range(5):
                    c = hf * 5 + ci
                    xk, xps = psbank("xg3", [0, 1, 5]) if cur_pass[0] == 2 else psbank("xps4", [0, 1, 6, 7])
                    for k in range(DC):
                        S.op(PE, (lambda e, xps=xps, k=k, ci=ci: e.matmul(xps[:], lhsT=winx[hf][:, k, ci * 128:(ci + 1) * 128], rhs=HNb[:, k, tsl(tt)],
                                                                        start=(k == 0), stop=(k == DC - 1))),
                             reads=[wk, hn_key(k, tt)], writes=[xk])
                    hi = hc_i[0] % 10; hc_i[0] += 1
                    hc, hct = HC[hi], HCT[hi]
                    hk = ("hc", hi)
                    S.op(POOL, (lambda e, hc=hc, c=c: e.tensor_scalar(out=hc[:, 0:3], in0=Hin[:, c, 0:3], scalar1=cw(0, c), scalar2=None, op0=ALU.mult)),
                         reads=[("halo", tt % 2, c), "vec"], writes=[hk])
                    S.op(POOL, (lambda e, hct=hct, c=c: e.tensor_scalar(out=hct[:, 0:2], in0=Hin[:, c, 1:3], scalar1=cw(1, c), scalar2=None, op0=ALU.mult)),
                         reads=[("halo", tt % 2, c), "vec"], writes=[("hct", hi)])
                    S.op(POOL, (lambda e, hc=hc, hct=hct: e.tensor_tensor(out=hc[:, 0:2], in0=hc[:, 0:2], in1=hct[:, 0:2], op=ALU.add)),
                         reads=[hk, ("hct", hi)], writes=[hk])
                    S.op(POOL, (lambda e, hct=hct, c=c: e.tensor_scalar(out=hct[:, 0:1], in0=Hin[:, c, 2:3], scalar1=cw(2, c), scalar2=None, op0=ALU.mult)),
                         reads=[("halo", tt % 2, c), "vec", hk], writes=[("hct", hi)])
                    S.op(POOL, (lambda e, hc=hc, hct=hct: e.tensor_tensor(out=hc[:, 0:1], in0=hc[:, 0:1], in1=hct[:, 0:1], op=ALU.add)),
                         reads=[hk, ("hct", hi)], writes=[hk])
                    S.op(ACT, (lambda e, xps=xps, ci=ci, c=c, XC=XC: e.activation(out=XC[:, ci, :], in_=xps[:], func=AF.Identity, scale=cw(3, c),
                                                                                bias=V("lru_conv_b", l2 * RC + c))),
                         reads=[xk, "vec"], writes=[("xc", u % 2, ci)])
                    S.op(ACT, (lambda e, xps=xps, c=c, Hout=Hout: e.activation(out=Hout[:, c, :], in_=xps[:, TT - 3:TT], func=AF.Identity)),
                         reads=[xk], writes=[("halo", (tt + 1) % 2, c)])
                    for kk in (2, 1, 0):
                        sh = 3 - kk
                        S.op(DVE, (lambda e, xps=xps, ci=ci, c=c, kk=kk, sh=sh, XC=XC: e.scalar_tensor_tensor(
                            out=XC[:, ci, sh:TT], in0=xps[:, 0:TT - sh], scalar=cw(kk, c), in1=XC[:, ci, sh:TT], op0=ALU.mult, op1=ALU.add)),
                             reads=[xk, "vec", ("xc", u % 2, ci)], writes=[("xc", u % 2, ci)])
                    S.op(DVE, (lambda e, ci=ci, hc=hc, XC=XC: e.tensor_tensor(out=XC[:, ci, 0:3], in0=XC[:, ci, 0:3], in1=hc[:, 0:3], op=ALU.add)),
                         reads=[hk, ("xc", u % 2, ci)], writes=[("xc", u % 2, ci)])
                    S.op(POOL, (lambda e, ci=ci, XC=XC, XCbb=XCbb: e.tensor_copy(XCbb[:, ci, :], XC[:, ci, :])), reads=[("xc", u % 2, ci)], writes=[("xcb", u % 2, ci)])
                    yield

            def stage2(u, pas, hf, tt, wing=None, wingk=None, wouts=None, woutk=None):
                wk = ("A", hf)
                XC = XC2[u % 2]; XCbb = XCb[u % 2]
                for grp in (([0, 1, 2], [3, 4]) if pas == 2 else ([0, 1, 2, 3, 4],)):
                    tl = {}
                    for cj in grp:
                        c = hf * 5 + cj
                        if pas == 2:
                            rk, r_ps = psbank("gate3", [2, 3, 4])
                            ik, i_ps = psbank("gate3", [2, 3, 4])
                        else:
                            rk, r_ps = psbank("ffu", [2, 3])
                            ik, i_ps = psbank("ffd", [4, 5])
                        nb = NBRS[cj]
                        for gi, (pk_, pb_) in enumerate(((rk, r_ps), (ik, i_ps))):
                            for n_, i in enumerate(nb):
                                S.op(PE, (lambda e, pb_=pb_, gi=gi, i=i, cj=cj, n_=n_, nb=nb, XCbb=XCbb: e.matmul(
                                    pb_[:], lhsT=gts[hf][:, gi * NGT + GT_IDX[(cj, i)], :], rhs=XCbb[:, i, :], start=(n_ == 0), stop=(n_ == len(nb) - 1))),
                                     reads=["gt", ("xcb", u % 2, i)], writes=[pk_])
                        Mk, M = tl_box[0].next()
                        Ik, I = tl_box[0].next()
                        Ak, A = tl_box[0].next()
                        tl[cj] = (Mk, M, Ik, I, Ak, A)
                        kw = dict(accum_out=TSUM[:, c, tt:tt + 1]) if pas == 1 else {}
                        S.op(ACT, (lambda e, M=M, r_ps=r_ps, c=c, kw=kw: e.activation(out=M, in_=r_ps[:], func=AF.Tanh, scale=0.5, bias=HBA(l2, c), **kw)),
                             reads=[rk, "sm_hb"], writes=[Mk] + ([("tsum", c)] if pas == 1 else []))
                        S.op(ACT, (lambda e, I=I, i_ps=i_ps, c=c: e.activation(out=I, in_=i_ps[:], func=AF.Tanh, scale=0.5, bias=HBX(l2, c))),
                             reads=[ik, "sm_hb"], writes=[Ik])
                        S.op(ACT, (lambda e, A=A, M=M, c=c: e.activation(out=A, in_=M, func=AF.Exp, scale=CH(l2, c), bias=CH(l2, c))),
                             reads=[Mk, "sm_c"], writes=[Ak])
                        S.op(ACT, (lambda e, M=M, c=c: e.activation(out=M, in_=M, func=AF.Exp, scale=CNEG(l2, c), bias=CNEG(l2, c))),
                             reads=[Mk, "sm_c"], writes=[Mk])
                        S.op(DVE, (lambda e, I=I, cj=cj, XC=XC: e.scalar_tensor_tensor(out=I, in0=I, scalar=1.0, in1=XC[:, cj, :], op0=ALU.add, op1=ALU.mult)),
                             reads=[Ik, ("xc", u % 2, cj)], writes=[Ik])
                        yield
                    for cj in grp:
                        Mk, M, Ik, I, Ak, A = tl[cj]
                        S.op(ACT, (lambda e, M=M: e.activation(out=M, in_=M, func=AF.Sqrt, scale=-1.0, bias=1.0)), reads=[Mk], writes=[Mk])
                    for cj in grp:
                        c = hf * 5 + cj
                        Mk, M, Ik, I, Ak, A = tl[cj]
                        S.op(DVE, (lambda e, I=I, M=M: e.scalar_tensor_tensor(out=I, in0=I, scalar=0.5, in1=M, op0=ALU.mult, op1=ALU.mult)),
                             reads=[Ik, Mk], writes=[Ik])
                        S.op(DVE, (lambda e, M=M, A=A, I=I, c=c: e.tensor_tensor_scan(out=M, data0=A, data1=I, initial=STATE[:, c:c + 1], op0=ALU.mult, op1=ALU.add)),
                             reads=[Ak, Ik, ("state", c), Mk], writes=[Mk])
                        S.op(POOL, (lambda e, M=M, c=c: e.tensor_copy(STATE[:, c:c + 1], M[:, TT - 1:TT])), reads=[Mk], writes=[("state", c)])
                    if pas == 2:
                        for cj in grp:
                            Mk, M, Ik, I, Ak, A = tl[cj]
                            gk_, g_ps = psbank("xg3", [0, 1, 5])
                            for k in range(DC):
                                S.op(PE, (lambda e, g_ps=g_ps, k=k, cj=cj: e.matmul(g_ps[:], lhsT=wing[:, k, cj * 128:(cj + 1) * 128], rhs=HNb[:, k, tsl(tt)],
                                                                                  start=(k == 0), stop=(k == DC - 1))),
                                     reads=[wingk, hn_key(k, tt)], writes=[gk_])
                            S.op(ACT, (lambda e, A=A, g_ps=g_ps: e.activation(out=A, in_=g_ps[:], func=AF.Gelu_apprx_tanh)), reads=[gk_, Ak], writes=[Ak])
                        for cj in grp:
                            Mk, M, Ik, I, Ak, A = tl[cj]
                            S.op(POOL, (lambda e, A=A, M=M, cj=cj: e.tensor_tensor(out=Yb[:, cj, :], in0=M, in1=A, op=ALU.mult)), reads=[Ak, Mk], writes=[("y", cj)])
                if pas == 2:
                    for j in range(DC):
                        ok_, o_ps = psbank("norm", [6, 7])
                        for ci in range(5):
                            S.op(PE, (lambda e, o_ps=o_ps, ci=ci, j=j: e.matmul(o_ps[:], lhsT=wouts[:, ci, j * 128:(j + 1) * 128], rhs=Yb[:, ci, :],
                                                                              start=(ci == 0), stop=(ci == 4))),
                                 reads=[woutk, ("y", ci)], writes=[ok_])
                        S.op(DVE, (lambda e, o_ps=o_ps, j=j: e.tensor_tensor(out=Hv[:, j, tsl(tt)], in0=o_ps[:], in1=Hv[:, j, tsl(tt)], op=ALU.add)),
                             reads=[ok_, hkeys(j, tt)], writes=[hkeys(j, tt)])

            def reset_run(init_state_ap):
                S.op(POOL, lambda e: e.tensor_copy(HALO2[0].rearrange("p c t -> p (c t)") if False else big_f32(smallb, 30), HALO0f), reads=["halo0"],
                     writes=[("halo", 0, c) for c in range(RC)])
                if init_state_ap is None:
                    S.op(POOL, lambda e: e.memset(STATE, 0.0), writes=[("state", c) for c in range(RC)])
                else:
                    S.op(POOL, lambda e: e.tensor_copy(STATE, init_state_ap), reads=["hinit"], writes=[("state", c) for c in range(RC)])

            cur_pass = [1]

            def run_pass(pas, wts):
                cur_pass[0] = pas
                units = [(hf, tt) for hf in range(2) for tt in range(NT)]
                for _ in stage1(0, *units[0]):
                    pass
                for u, (hf, tt) in enumerate(units):
                    if tt == 0:
                        load_gt(hf)
                        w_ = wts(hf) if pas == 2 else (None,) * 4
                    g1 = stage1(u + 1, *units[u + 1]) if u + 1 < len(units) else iter(())
                    if not INTERLEAVE:
                        for _ in g1:
                            pass
                    for _ in stage2(u, pas, hf, tt, *w_):
                        next(g1, None)
                    for _ in g1:
                        pass

            reset_run(None)
            run_pass(1, None)
            allts = [("tsum", c) for c in range(RC)]
            S.op(DVE, lambda e: e.tensor_reduce(out=RS1, in_=TSUM, axis=mybir.AxisListType.X, op=ALU.add), reads=allts, writes=["rs1"])
            S.op(DVE, lambda e: e.scalar_tensor_tensor(out=RS1, in0=RS1, scalar=float(T), in1=sm[:, 20 + l2 * RC: 20 + (l2 + 1) * RC], op0=ALU.add, op1=ALU.mult),
                 reads=["rs1", "sm_c"], writes=["rs1"])
            S.op(ACT, lambda e: e.activation(out=EX2[:, 0:RC], in_=RS1, func=AF.Exp), reads=["rs1", "ex1g", "halo0"], writes=["ex2a", "ex1"])
            S.op(POOL, lambda e: e.tensor_copy(EX2[:, RC:2 * RC], STATE), reads=[("state", c) for c in range(RC)] + ["ex1g", "halo0"], writes=["ex2b", "ex1"])
            exchange(EX2, 128, ["ex2a", "ex2b"], EX2G, "ex1g")
            S.op(DVE, lambda e: e.memset(HINIT, 0.0), writes=["hinit"])
            for j in range(3):
                S.op(DVE, (lambda e, j=j: e.tensor_tensor(out=TSM, in0=EX2G[:, j, 0:RC], in1=HINIT, op=ALU.mult)), reads=["ex1g", "hinit"], writes=["tsm"])
                S.op(DVE, (lambda e, j=j: e.tensor_tensor(out=TSM, in0=TSM, in1=EX2G[:, j, RC:2 * RC], op=ALU.add)), reads=["ex1g", "tsm"], writes=["tsm"])
                S.op(DVE, (lambda e, j=j: e.tensor_tensor(out=TSM, in0=TSM, in1=HINIT, op=ALU.subtract)), reads=["hinit", "tsm"], writes=["tsm"])
                S.op(DVE, (lambda e, j=j: e.scalar_tensor_tensor(out=HINIT, in0=TSM, scalar=V("msk", j), in1=HINIT, op0=ALU.mult, op1=ALU.add)),
                     reads=["tsm", "hinit", "vec"], writes=["hinit"])
            woutv = wout_d[l2].rearrange("(k p) d -> p k d", p=128)
            reset_run(HINIT)
            S.op(DVE, lambda e: e.memset(sm[:, 504:505], 0.0), writes=xkeys)
            ring_release(); ring_release()
            tl_box[0] = Rot("tl", tiles)

            def load_w2(hf):
                wingk, sl1 = ring_piece([(lambda s: s[:, 0:5120].rearrange("p (k f) -> p k f", k=8), winv[:, :, hf * 640:(hf + 1) * 640])])
                wing = sl1[:, 0:5120].rearrange("p (k f) -> p k f", k=8)
                woutk, sl2 = ring_piece([(lambda s: s[:, 0:5120].rearrange("p (k d) -> p k d", k=5), woutv[:, hf * 5:(hf + 1) * 5, :])])
                wouts = sl2[:, 0:5120].rearrange("p (k d) -> p k d", k=5)
                return (wing, wingk, wouts, woutk)
            run_pass(2, load_w2)
            S.op(DVE, lambda e: e.memset(sm[:, 503:504], 0.0), writes=[sk_] + tlkeys)
            ring_release()

        for l in layers:
            if "1" in phases:
                fence()
                ffn(l, f1g, f1u, f1d, "ffn1_norm")
            if "m" in phases:
                fence()
                if l % 2 == 0:
                    lru_layer(l)
                else:
                    pool_layer(l)
            if "2" in phases:
                fence()
                ffn(l, f2g, f2u, f2d, "ffn2_norm", mid_hook=(lambda l=l: ple_prefetch(l)) if "p" in phases else None)
            if "p" in phases:
                fence()
                if l not in ple_bufs:
                    ple_prefetch(l)
                ple(l)
        fence()

        ov = out_d.rearrange("(c p) t -> p c t", p=128)
        OUTS = [BIGreg[:, i * 4096:(i + 1) * 4096].rearrange("p (c t) -> p c t", c=DC) for i in range(2)]
        for tt in range(NT):
            ob = OUTS[tt % 2]
            if final:
                norm("final_norm", 0, lambda c, tt_, ob=ob: ob[:, c, :], lambda c, tt_: ("outs", tt_ % 2), tts=(tt,))
            else:
                for c in range(DC):
                    S.op(POOL, (lambda e, ob=ob, c=c, tt=tt: e.tensor_copy(ob[:, c, :], Hv[:, c, tsl(tt)])), reads=[hkeys(c, tt)], writes=[("outs", tt % 2)])
            S.dma(SP, (lambda e, ob=ob, tt=tt: e.dma_start(out=ov[:, :, tsl(tt)], in_=ob)), osem[tt % 2], reads=[("outs", tt % 2)],
                  writes=[("outd", tt), ("outs", tt % 2)])
        S.barrier(SP, [("outd", tt) for tt in range(NT)])
        S.emit(block, esem)
        build.nops = len(S.ops)
    return nc


VEC_SPECS.append(("eps", 1))
VOFF["eps"] = NV
NV += 1


def _prep_inputs(inp):
    x = np.asarray(inp["x"], np.float32)
    p = np.asarray(inp["p"], np.float32)
    gt = np.zeros((2, 2, 128, 2 * NGT, 128), np.float32)
    for l2 in range(2):
        for gi, nm in enumerate(("lru_w_a", "lru_w_x")):
            w = np.asarray(inp[nm], np.float32)[l2]
            dense = np.zeros((DR, DR), np.float32)
            for h in range(16):
                dense[h * 80:(h + 1) * 80, h * 80:(h + 1) * 80] = w[h]
            for hf in range(2):
                for (j, i), n in GT_IDX.items():
                    gt[l2, hf, :, gi * NGT + n, :] = dense[(hf * 5 + i) * 128:(hf * 5 + i + 1) * 128, (hf * 5 + j) * 128:(hf * 5 + j + 1) * 128]
    common_vec = {}
    for nm in ("ffn1_norm", "mix_norm", "ffn2_norm", "ple_norm", "final_norm", "pool_b", "pool_scale",
               "lru_conv_w", "lru_conv_b", "lru_b_a", "lru_b_x", "lru_a_param"):
        common_vec[nm] = _colpack(np.asarray(inp[nm], np.float32))
    shared = {k: np.ascontiguousarray(np.asarray(inp[k], np.float32)) for k in
              ("ffn1_w_gate", "ffn1_w_up", "ffn1_w_down", "ffn2_w_gate", "ffn2_w_up", "ffn2_w_down",
               "lru_w_in", "lru_w_out", "pool_w", "ple_w_gate", "ple_w_proj")}
    shared["lru_gt"] = gt
    in_maps = []
    for c in range(NCORE):
        b, k = divmod(c, 4)
        vecs = np.zeros((128, NV), np.float32)
        for nm, arr in common_vec.items():
            vecs[:, VOFF[nm]:VOFF[nm] + arr.shape[1]] = arr
        sel = np.zeros(4, np.float32)
        if k > 0:
            sel[k - 1] = 1.0
        msk = np.zeros(4, np.float32)
        msk[:k] = 1.0
        vecs[:, VOFF["sel"]:VOFF["sel"] + 4] = sel[None, :]
        vecs[:, VOFF["msk"]:VOFF["msk"] + 4] = msk[None, :]
        ic = np.zeros((4, 16), np.float32)
        for g in range(4):
            win = 2 << g
            for t in range(16):
                ic[g, t] = np.float32(1.0) / np.float32(min(t + 1, win) if k == 0 else win)
        vecs[:, VOFF["invcnt"]:VOFF["invcnt"] + 64] = ic.reshape(1, 64)
        vecs[:, VOFF["eps"]] = EPS
        m = dict(shared)
        m["xT"] = np.ascontiguousarray(x[b, k * T:(k + 1) * T, :].T)
        m["pT"] = np.ascontiguousarray(p[:, b, k * T:(k + 1) * T, :].transpose(0, 2, 1))
        m["vecs"] = vecs
        in_maps.append(m)
    return in_maps


_NC_CACHE = {}


def kernel(**inputs):
    in_maps = _prep_inputs(inputs)
    if "nc" not in _NC_CACHE:
        _NC_CACHE["nc"] = build()
    nc = _NC_CACHE["nc"]
    res = run_bass_kernel_spmd(nc, in_maps, core_ids=list(range(NCORE)))
    out = np.empty((2, 4 * T, D), np.float32)
    for c in range(NCORE):
        b, k = divmod(c, 4)
        out[b, k * T:(k + 1) * T, :] = res.results[c]["outT"].T
    return out
```

```python
import numpy as np
from contextlib import ExitStack
import concourse.bass as bass
import concourse.mybir as mybir
from concourse.bass_utils import run_bass_kernel_spmd

F32 = mybir.dt.float32
BF16 = mybir.dt.bfloat16
AF = mybir.ActivationFunctionType
ALU = mybir.AluOpType

PE, ACT, DVE, POOL, SP = "pe", "act", "dve", "pool", "sp"
ENGS = (PE, ACT, DVE, POOL, SP)

NCORE = 8
INTERLEAVE = False
D = 1024
DC = 8
T = 2048
TT = 512
NT = T // TT
FF = 2816
FC = 22
DR = 1280
RC = 10
DEPTH = 4
EPS = 1e-6
NBRS = {0: [0, 1], 1: [0, 1, 2], 2: [1, 2, 3], 3: [2, 3, 4], 4: [3, 4]}
GT_IDX = {}
_n = 0
for _j in range(5):
    for _i in NBRS[_j]:
        GT_IDX[(_j, _i)] = _n
        _n += 1
NGT = _n


class Op:
    __slots__ = ("eng", "fn", "deps", "is_dma", "sem", "target", "seq", "idx")


class Sched:
    def __init__(self, nc):
        self.nc = nc
        self.ops = []
        self.last_w = {}
        self.readers = {}
        self.cnt = {e: 0 for e in ENGS}
        self.dma_sem_total = {}
        self.phase_op = None
        self.last_eng_op = {}
        self.dmas_since = []

    def fence(self, fn):
        o = Op()
        o.eng = DVE
        o.fn = fn
        o.idx = len(self.ops)
        o.deps = set(self.last_eng_op.values()) | set(self.dmas_since)
        o.is_dma = False
        self.cnt[DVE] += 1
        o.seq = self.cnt[DVE]
        o.sem = None
        o.target = None
        self.ops.append(o)
        self.phase_op = o.idx
        self.last_eng_op[DVE] = o.idx
        self.dmas_since = []
        return o

    def _mk(self, eng, fn, reads, writes, nophase=False):
        o = Op()
        o.eng = eng
        o.fn = fn
        o.idx = len(self.ops)
        deps = set()
        if self.phase_op is not None and not nophase:
            deps.add(self.phase_op)
        for k in reads:
            w = self.last_w.get(k)
            if w is not None:
                deps.add(w)
        for k in writes:
            w = self.last_w.get(k)
            if w is not None:
                deps.add(w)
            for r in self.readers.get(k, ()):
                deps.add(r)
        deps.discard(o.idx)
        o.deps = deps
        for k in reads:
            self.readers.setdefault(k, []).append(o.idx)
        for k in writes:
            self.last_w[k] = o.idx
            self.readers[k] = []
        self.ops.append(o)
        return o

    def op(self, eng, fn, reads=(), writes=()):
        o = self._mk(eng, fn, reads, writes)
        o.is_dma = False
        self.cnt[eng] += 1
        o.seq = self.cnt[eng]
        o.sem = None
        o.target = None
        self.last_eng_op[eng] = o.idx
        return o

    def barrier(self, eng, reads):
        o = self._mk(eng, None, reads, ())
        o.is_dma = False
        o.seq = self.cnt[eng]
        o.sem = None
        o.target = None
        return o

    def dma(self, eng, fn, sem, reads=(), writes=(), inc=16, nophase=False):
        o = self._mk(eng, fn, reads, writes, nophase=nophase)
        self.dmas_since.append(o.idx)
        o.is_dma = True
        o.sem = sem
        key = id(sem)
        self.dma_sem_total[key] = self.dma_sem_total.get(key, 0) + inc
        o.target = self.dma_sem_total[key]
        o.seq = inc
        return o

    def emit(self, block, esem):
        ops = self.ops
        per = {e: [o for o in ops if o.eng == e] for e in ENGS}

        def run(eng_name, engine):
            waited = {}
            for o in per[eng_name]:
                need = {}
                for d in o.deps:
                    p = ops[d]
                    if p.is_dma:
                        s, v = p.sem, p.target
                    else:
                        if p.eng == eng_name:
                            if eng_name in (PE, SP):
                                continue
                        s, v = esem[p.eng], p.seq
                    k = id(s)
                    if need.get(k, (None, 0))[1] < v:
                        need[k] = (s, v)
                for k, (s, v) in need.items():
                    if waited.get(k, 0) >= v:
                        continue
                    engine.wait_ge(s, v)
                    waited[k] = v
                if o.fn is None:
                    continue
                ins = o.fn(engine)
                if o.is_dma:
                    ins.then_inc(o.sem, o.seq)
                else:
                    ins.then_inc(esem[eng_name], 1)

        @block.tensor
        def _(e):
            run(PE, e)

        @block.scalar
        def _(e):
            run(ACT, e)

        @block.vector
        def _(e):
            run(DVE, e)

        @block.gpsimd
        def _(e):
            run(POOL, e)

        @block.sync
        def _(e):
            run(SP, e)


VEC_SPECS = [
    ("ffn1_norm", DEPTH * DC), ("mix_norm", DEPTH * DC), ("ffn2_norm", DEPTH * DC), ("ple_norm", DEPTH * DC),
    ("final_norm", DC), ("pool_b", 2 * DC), ("pool_scale", 2 * DC),
    ("lru_conv_w", 2 * 4 * RC), ("lru_conv_b", 2 * RC), ("lru_b_a", 2 * RC), ("lru_b_x", 2 * RC),
    ("lru_a_param", 2 * RC),
    ("sel", 4), ("msk", 4), ("invcnt", 4 * 16),
]
VOFF = {}
_o = 0
for _nm, _n2 in VEC_SPECS:
    VOFF[_nm] = _o
    _o += _n2
NV = _o


def _colpack(v):
    v = np.asarray(v, np.float32)
    C = v.shape[-1] // 128
    lead = int(np.prod(v.shape[:-1])) if v.ndim > 1 else 1
    return np.ascontiguousarray(v.reshape(lead, C, 128).transpose(2, 0, 1).reshape(128, lead * C))


def build(layers=(0, 1, 2, 3), final=True, stop_after=None, phases="1m2p"):
    nc = bass.Bass("TRN2", target_bir_lowering=False)
    dt_in = lambda name, shape: nc.dram_tensor(name, shape, F32, kind="ExternalInput").ap()
    xT_d = dt_in("xT", [D, T])
    pT_d = dt_in("pT", [DEPTH, 256, T])
    vec_d = dt_in("vecs", [128, NV])
    f1g = dt_in("ffn1_w_gate", [DEPTH, D, FF]); f1u = dt_in("ffn1_w_up", [DEPTH, D, FF]); f1d = dt_in("ffn1_w_down", [DEPTH, FF, D])
    f2g = dt_in("ffn2_w_gate", [DEPTH, D, FF]); f2u = dt_in("ffn2_w_up", [DEPTH, D, FF]); f2d = dt_in("ffn2_w_down", [DEPTH, FF, D])
    win_d = dt_in("lru_w_in", [2, D, 2 * DR])
    wout_d = dt_in("lru_w_out", [2, DR, D])
    gt_d = dt_in("lru_gt", [2, 2, 128, 2 * NGT, 128])
    poolw_d = dt_in("pool_w", [2, 4, 256, 256])
    pleg_d = dt_in("ple_w_gate", [DEPTH, D, D])
    plep_d = dt_in("ple_w_proj", [DEPTH, 256, D])
    out_d = nc.dram_tensor("outT", [D, T], F32, kind="ExternalOutput").ap()
    ex_widths = []
    for l_ in layers:
        ex_widths += [128, 128] if l_ % 2 == 0 else [128]
    ex_bufs = [(nc.dram_tensor(f"ex_in{i}", [128, w], F32).ap(), nc.dram_tensor(f"ex_out{i}", [4 * 128, w], F32).ap())
               for i, w in enumerate(ex_widths)]

    with ExitStack() as es:
        H_W = DC * T
        HN_W = DC * T // 2
        BIG_W = 12 * T // 2
        SLOT_W = 2560
        NSLOT = 3
        TMP_W = 7424
        arena = es.enter_context(nc.sbuf_tensor("arena", [128, H_W + HN_W + BIG_W + NSLOT * SLOT_W + TMP_W], F32))
        vec = es.enter_context(nc.sbuf_tensor("vec", [128, NV], F32))
        sm = es.enter_context(nc.sbuf_tensor("sm", [128, 512], F32))
        onesb = es.enter_context(nc.sbuf_tensor("onesb", [128, 128], BF16))
        o0 = 0
        Hreg = arena[:, o0:o0 + H_W]; o0 += H_W
        HNreg = arena[:, o0:o0 + HN_W]; hb0 = o0; o0 += HN_W
        BIGreg = arena[:, o0:o0 + BIG_W]; big0 = o0; o0 += BIG_W
        SLOTreg = [arena[:, o0 + i * SLOT_W: o0 + (i + 1) * SLOT_W] for i in range(NSLOT)]; o0 += NSLOT * SLOT_W
        tmp0 = o0
        Hv = Hreg.rearrange("p (c t) -> p c t", c=DC)
        HNb = HNreg.bitcast(BF16).rearrange("p (c t) -> p c t", c=DC)
        HIDb = BIGreg.bitcast(BF16).rearrange("p (c t) -> p c t", c=12)
        slotb = [s.bitcast(BF16) for s in SLOTreg]

        def tmp_f32(off, n):
            return arena[:, tmp0 + off: tmp0 + off + n]

        def big_f32(off, n):
            return arena[:, big0 + off: big0 + off + n]

        ps = [es.enter_context(nc.psum_tensor(f"ps{i}", [128, TT], F32)) for i in range(8)]
        esem = {e: es.enter_context(nc.semaphore(f"s_{e}")) for e in ENGS}
        ssem = [es.enter_context(nc.semaphore(f"slot{i}")) for i in range(NSLOT)]
        xsem = [es.enter_context(nc.semaphore(f"xs{i}")) for i in range(DC)]
        osem = [es.enter_context(nc.semaphore(f"os{i}")) for i in range(2)]
        vsem = es.enter_context(nc.semaphore("vs"))
        psem = es.enter_context(nc.semaphore("psm"))
        asem = [es.enter_context(nc.semaphore(f"as{i}")) for i in range(2)]
        exsem = [es.enter_context(nc.semaphore(f"ex{i}")) for i in range(3)]
        block = es.enter_context(nc.Block())
        S = Sched(nc)

        V = lambda name, i=0: vec[:, VOFF[name] + i: VOFF[name] + i + 1]

        def tsl(tt):
            return slice(tt * TT, (tt + 1) * TT)

        class Rot:
            def __init__(self, name, aps):
                self.name = name; self.aps = aps; self.i = 0

            def next(self):
                k = self.i % len(self.aps); self.i += 1
                return (self.name, k), self.aps[k]

        psrot = {}

        def psbank(role, banks):
            if role not in psrot:
                psrot[role] = [0, banks]
            r = psrot[role]
            b = r[1][r[0] % len(r[1])]; r[0] += 1
            return ("ps", b), ps[b]

        ring_order = list(range(NSLOT))
        ring_held = []

        def ring_piece(dmas):
            s = ring_order.pop(0); ring_order.append(s)
            key = ("slot", s)
            for dst_fn, src in dmas:
                S.dma(POOL, (lambda e, dst_fn=dst_fn, src=src, s=s: e.dma_start(out=dst_fn(slotb[s]), in_=src)),
                      ssem[s], writes=[key], nophase=True)
            return key, slotb[s]

        def ring_reserve():
            s = ring_order.pop(0)
            ring_held.append(s)
            return ("slot", s), SLOTreg[s]

        def ring_release():
            ring_order.append(ring_held.pop())

        S.dma(SP, lambda e: e.dma_start(out=vec[:], in_=vec_d), vsem, writes=["vec"])
        S.op(POOL, lambda e: e.memset(onesb[:], 1.0), writes=["ones"])
        xv = xT_d.rearrange("(c p) t -> p c t", p=128)
        for c in range(DC):
            S.dma(SP, (lambda e, c=c: e.dma_start(out=Hv[:, c, :], in_=xv[:, c, :])), xsem[c],
                  writes=[("h", c, tt) for tt in range(NT)])

        CNEG = lambda l2, c: sm[:, l2 * RC + c: l2 * RC + c + 1]
        CH = lambda l2, c: sm[:, 20 + l2 * RC + c: 20 + l2 * RC + c + 1]
        PBS = lambda l2, c: sm[:, 40 + l2 * DC + c: 40 + l2 * DC + c + 1]
        HBA = lambda l2, c: sm[:, 56 + l2 * RC + c: 56 + l2 * RC + c + 1]
        HBX = lambda l2, c: sm[:, 76 + l2 * RC + c: 76 + l2 * RC + c + 1]
        ap_ = vec[:, VOFF["lru_a_param"]: VOFF["lru_a_param"] + 2 * RC]
        S.op(ACT, lambda e: e.activation(out=sm[:, 0:20], in_=ap_, func=AF.Exp, scale=-1.0), reads=["vec"], writes=["sm_c"])
        S.op(ACT, lambda e: e.activation(out=sm[:, 0:20], in_=sm[:, 0:20], func=AF.Ln, bias=1.0), reads=["sm_c"], writes=["sm_c"])
        S.op(DVE, lambda e: e.tensor_scalar(out=sm[:, 20:40], in0=sm[:, 0:20], scalar1=-4.0, scalar2=None, op0=ALU.mult), reads=["sm_c"], writes=["sm_c2"])
        S.op(DVE, lambda e: e.tensor_scalar(out=sm[:, 0:20], in0=sm[:, 0:20], scalar1=-8.0, scalar2=None, op0=ALU.mult), reads=["sm_c", "sm_c2"], writes=["sm_c"])
        S.op(DVE, lambda e: e.tensor_tensor(out=sm[:, 40:56], in0=vec[:, VOFF["pool_b"]:VOFF["pool_b"] + 16],
                                            in1=vec[:, VOFF["pool_scale"]:VOFF["pool_scale"] + 16], op=ALU.mult), reads=["vec"], writes=["sm_pbs"])
        S.op(DVE, lambda e: e.tensor_scalar(out=sm[:, 56:76], in0=vec[:, VOFF["lru_b_a"]:VOFF["lru_b_a"] + 20], scalar1=0.5, scalar2=None, op0=ALU.mult), reads=["vec"], writes=["sm_hb"])
        S.op(DVE, lambda e: e.tensor_scalar(out=sm[:, 76:96], in0=vec[:, VOFF["lru_b_x"]:VOFF["lru_b_x"] + 20], scalar1=0.5, scalar2=None, op0=ALU.mult), reads=["vec", "sm_hb"], writes=["sm_hb"])

        hkeys = lambda c, tt: ("h", c, tt)

        def fence():
            S.fence(lambda e: e.memset(sm[:, 500:501], 0.0))

        sq_rot = Rot("sq", [tmp_f32(i * 256, 256).bitcast(BF16) for i in range(2)])
        rs_rot = Rot("rs", [tmp_f32(512 + i * 512, 512) for i in range(2)])
        TMPB = 512 + 1024

        def norm(gname, gidx, dst_fn, dst_key_fn, tts=range(NT)):
            for tt in tts:
                pk, pb = psbank("norm", [6, 7])
                for c in range(DC):
                    sk, sq = sq_rot.next()
                    S.op(ACT, (lambda e, sq=sq, c=c, tt=tt: e.activation(out=sq, in_=Hv[:, c, tsl(tt)], func=AF.Square)),
                         reads=[hkeys(c, tt)], writes=[sk])
                    S.op(PE, (lambda e, sq=sq, pb=pb, c=c: e.matmul(pb[:], lhsT=onesb[:], rhs=sq, start=(c == 0), stop=(c == DC - 1))),
                         reads=[sk, "ones"], writes=[pk])
                rk, rs = rs_rot.next()
                S.op(ACT, (lambda e, rs=rs, pb=pb: e.activation(out=rs, in_=pb[:], func=AF.Ln, scale=1.0 / D, bias=V("eps"))),
                     reads=[pk, "vec"], writes=[rk])
                S.op(ACT, (lambda e, rs=rs: e.activation(out=rs, in_=rs, func=AF.Exp, scale=-0.5)), reads=[rk], writes=[rk])
                for c in range(DC):
                    S.op(DVE, (lambda e, rs=rs, c=c, tt=tt: e.scalar_tensor_tensor(
                        out=dst_fn(c, tt), in0=Hv[:, c, tsl(tt)], scalar=V(gname, gidx * DC + c), in1=rs, op0=ALU.mult, op1=ALU.mult)),
                         reads=[hkeys(c, tt), rk, "vec"], writes=[dst_key_fn(c, tt)])

        hn_dst = lambda c, tt: HNb[:, c, tsl(tt)]
        hn_key = lambda c, tt: ("hn", c, tt)

        sg_rot = Rot("sg", [tmp_f32(TMPB + i * 512, 512) for i in range(3)])
        FFN_GROUPS = [list(range(0, 12)), list(range(12, 22))]

        def ffn(l, wg, wu, wd, nname, mid_hook=None):
            normed = set()

            def need(tt):
                for t_ in (tt, tt + 1):
                    if t_ < NT and t_ not in normed:
                        normed.add(t_)
                        norm(nname, l, hn_dst, hn_key, tts=(t_,))
            wgv = wg[l].rearrange("(k p) f -> p k f", p=128)
            wuv = wu[l].rearrange("(k p) f -> p k f", p=128)
            for gi_, grp in enumerate(FFN_GROUPS):
                if gi_ == 1 and mid_hook is not None:
                    mid_hook()
                for pi in range(0, len(grp), 2):
                    f0 = grp[pi]
                    key, sl = ring_piece([
                        (lambda s: s[:, 0:2048].rearrange("p (k f) -> p k f", k=8), wgv[:, :, f0 * 128:(f0 + 2) * 128]),
                        (lambda s: s[:, 2048:4096].rearrange("p (k f) -> p k f", k=8), wuv[:, :, f0 * 128:(f0 + 2) * 128]),
                    ])
                    wgs = sl[:, 0:2048].rearrange("p (k f) -> p k f", k=8)
                    wus = sl[:, 2048:4096].rearrange("p (k f) -> p k f", k=8)
                    for fo in range(2):
                        fi = pi + fo
                        for tt in range(NT):
                            need(tt)
                            gk, gb = psbank("ffg", [0, 1])
                            uk, ub = psbank("ffu", [2, 3])
                            for k in range(DC):
                                S.op(PE, (lambda e, gb=gb, k=k, fo=fo, tt=tt, wgs=wgs: e.matmul(
                                    gb[:], lhsT=wgs[:, k, fo * 128:(fo + 1) * 128], rhs=HNb[:, k, tsl(tt)], start=(k == 0), stop=(k == DC - 1))),
                                     reads=[key, hn_key(k, tt)], writes=[gk])
                            for k in range(DC):
                                S.op(PE, (lambda e, ub=ub, k=k, fo=fo, tt=tt, wus=wus: e.matmul(
                                    ub[:], lhsT=wus[:, k, fo * 128:(fo + 1) * 128], rhs=HNb[:, k, tsl(tt)], start=(k == 0), stop=(k == DC - 1))),
                                     reads=[key, hn_key(k, tt)], writes=[uk])
                            sk, sg = sg_rot.next()
                            S.op(ACT, (lambda e, sg=sg, gb=gb: e.activation(out=sg, in_=gb[:], func=AF.Silu)), reads=[gk], writes=[sk])
                            S.op(DVE, (lambda e, sg=sg, ub=ub, fi=fi, tt=tt: e.tensor_tensor(out=HIDb[:, fi, tsl(tt)], in0=sg, in1=ub[:], op=ALU.mult)),
                                 reads=[sk, uk], writes=[("hid", fi, tt)])
                nf = len(grp)
                wdv = wd[l][grp[0] * 128:(grp[0] + nf) * 128, :].rearrange("(k p) d -> p k d", p=128)
                for jp in range(0, DC, 2):
                    key, sl = ring_piece([(lambda s, nf=nf: s[:, 0:nf * 256].rearrange("p (k d) -> p k d", k=nf), wdv[:, :, jp * 128:(jp + 2) * 128])])
                    wds = sl[:, 0:nf * 256].rearrange("p (k d) -> p k d", k=nf)
                    for jo in range(2):
                        j = jp + jo
                        for tt in range(NT):
                            dk, db = psbank("ffd", [4, 5])
                            for fi in range(nf):
                                S.op(PE, (lambda e, db=db, fi=fi, jo=jo, tt=tt, wds=wds, nf=nf: e.matmul(
                                    db[:], lhsT=wds[:, fi, jo * 128:(jo + 1) * 128], rhs=HIDb[:, fi, tsl(tt)], start=(fi == 0), stop=(fi == nf - 1))),
                                     reads=[key, ("hid", fi, tt)], writes=[dk])
                            S.op(DVE, (lambda e, db=db, j=j, tt=tt: e.scalar_tensor_tensor(
                                out=Hv[:, j, tsl(tt)], in0=db[:], scalar=0.5, in1=Hv[:, j, tsl(tt)], op0=ALU.mult, op1=ALU.add)),
                                 reads=[dk, hkeys(j, tt)], writes=[hkeys(j, tt)])

        ple_bufs = {}

        def ple_prefetch(l):
            PTb = tmp_f32(3072, 2048).bitcast(BF16).rearrange("p (c t) -> p c t", c=2)
            wps = tmp_f32(5120, 1024).bitcast(BF16).rearrange("p (k d) -> p k d", k=2)
            S.dma(POOL, lambda e: e.dma_start(out=PTb, in_=pT_d[l].rearrange("(c p) t -> p c t", p=128)), psem, writes=["pt"])
            S.dma(POOL, lambda e: e.dma_start(out=wps, in_=plep_d[l].rearrange("(k p) d -> p k d", p=128)), asem[0], writes=["wproj"])
            ple_bufs[l] = (PTb, wps, "wproj")

        def ple(l):
            normed = set()

            def need(tt):
                for t_ in (tt, tt + 1):
                    if t_ < NT and t_ not in normed:
                        normed.add(t_)
                        norm("ple_norm", l, hn_dst, hn_key, tts=(t_,))
            PTb, wps, kp = ple_bufs[l]
            wgv = pleg_d[l].rearrange("(k p) d -> p k d", p=128)
            pg_rot = Rot("pg", [tmp_f32(TMPB + i * 512, 512) for i in range(3)])
            for jp in range(0, DC, 2):
                key, sl = ring_piece([(lambda s: s[:, 0:2048].rearrange("p (k d) -> p k d", k=8), wgv[:, :, jp * 128:(jp + 2) * 128])])
                wgs = sl[:, 0:2048].rearrange("p (k d) -> p k d", k=8)
                for jo in range(2):
                    j = jp + jo
                    for tt in range(NT):
                        need(tt)
                        ak, ab = psbank("ffg", [0, 1])
                        bk, bb = psbank("ffu", [2, 3])
                        for k in range(DC):
                            S.op(PE, (lambda e, ab=ab, k=k, jo=jo, tt=tt, wgs=wgs: e.matmul(
                                ab[:], lhsT=wgs[:, k, jo * 128:(jo + 1) * 128], rhs=HNb[:, k, tsl(tt)], start=(k == 0), stop=(k == DC - 1))),
                                 reads=[key, hn_key(k, tt)], writes=[ak])
                        for k in range(2):
                            S.op(PE, (lambda e, bb=bb, k=k, j=j, tt=tt: e.matmul(
                                bb[:], lhsT=wps[:, k, j * 128:(j + 1) * 128], rhs=PTb[:, k, tsl(tt)], start=(k == 0), stop=(k == 1))),
                                 reads=[kp, "pt"], writes=[bk])
                        gk, g = pg_rot.next()
                        S.op(ACT, (lambda e, g=g, ab=ab: e.activation(out=g, in_=ab[:], func=AF.Sigmoid)), reads=[ak], writes=[gk])
                        S.op(DVE, (lambda e, g=g, bb=bb: e.tensor_tensor(out=g, in0=g, in1=bb[:], op=ALU.mult)), reads=[gk, bk], writes=[gk])
                        S.op(POOL, (lambda e, g=g, j=j, tt=tt: e.tensor_tensor(out=Hv[:, j, tsl(tt)], in0=Hv[:, j, tsl(tt)], in1=g, op=ALU.add)),
                             reads=[gk, hkeys(j, tt)], writes=[hkeys(j, tt)])

        ex_cnt = {"i": 0}

        def exchange(src_ap, width, src_keys, dst_ap, dst_key):
            i = ex_cnt["i"]; ex_cnt["i"] += 1
            exi, exo = ex_bufs[i]
            assert exi.shape[1] == width, (exi.shape, width)
            S.dma(POOL, lambda e: e.dma_start(out=exi, in_=src_ap), exsem[0], reads=src_keys, writes=[("exin", i)])
            S.dma(POOL, lambda e: e.collective_compute("AllGather", ALU.bypass, replica_groups=[[0, 1, 2, 3], [4, 5, 6, 7]],
                                                       ins=[exi], outs=[exo]), exsem[1], reads=[("exin", i)], writes=[("exout", i)], inc=1)
            S.dma(POOL, lambda e: e.dma_start(out=dst_ap, in_=exo.rearrange("(r p) f -> p r f", p=128)), exsem[2],
                  reads=[("exout", i)], writes=[dst_key])

        def select_prev(dst_ap, g_ap, width, dst_keys, g_key, eng=DVE):
            S.op(DVE, lambda e: e.tensor_scalar(out=dst_ap, in0=g_ap[:, 0, :], scalar1=V("sel", 0), scalar2=None, op0=ALU.mult),
                 reads=[g_key, "vec"], writes=dst_keys)
            for j in range(1, 4):
                S.op(DVE, (lambda e, j=j: e.scalar_tensor_tensor(out=dst_ap, in0=g_ap[:, j, :], scalar=V("sel", j), in1=dst_ap, op0=ALU.mult, op1=ALU.add)),
                     reads=[g_key, "vec"] + dst_keys, writes=dst_keys)

        def pool_layer(l):
            l2 = l // 2
            HW = 16 + T
            HN32 = arena[:, hb0: hb0 + DC * HW].rearrange("p (c t) -> p c t", c=DC)
            extra0 = hb0 + DC * HW - big0
            EXS = big_f32(extra0, 128)
            EXG = big_f32(extra0 + 128, 512).rearrange("p (r f) -> p r f", r=4)
            S.op(POOL, lambda e: e.memset(EXS, 0.0), writes=["pexs"])
            hn32_key = lambda c, tt: ("hn32", c, tt)
            norm("mix_norm", l, lambda c, tt: HN32[:, c, 16 + tt * TT: 16 + (tt + 1) * TT], hn32_key, tts=(3, 0, 1, 2))
            for c in range(DC):
                S.op(POOL, (lambda e, c=c: e.tensor_copy(EXS[:, c * 15:(c + 1) * 15], HN32[:, c, 16 + T - 15: 16 + T])),
                     reads=[hn32_key(c, 3)], writes=["pexs"])
            exchange(EXS, 128, ["pexs"], EXG, "pexg")
            HALO = big_f32(extra0 + 640, 120)
            select_prev(HALO, EXG[:, :, 0:120], 120, ["phalo"], "pexg")
            for c in range(DC):
                S.op(POOL, (lambda e, c=c: e.tensor_copy(HN32[:, c, 1:16], HALO[:, c * 15:(c + 1) * 15])), reads=["phalo"], writes=[("hn32h", c)])
            kw, slw = ring_piece([(lambda s: s[:, 0:2048].rearrange("p (g k d) -> p g k d", g=4, k=2),
                                   poolw_d[l2].rearrange("g (k p) d -> p g k d", p=128))])
            pw = slw[:, 0:2048].rearrange("p (g k d) -> p g k d", g=4, k=2)
            WB = 528
            wk_rot = Rot("pwk", [tmp_f32(TMPB + i * WB, WB) for i in range(3)])
            U_rot = [tmp_f32(TMPB + 3 * WB + i * 2048, 2048).bitcast(BF16).rearrange("p (c t) -> p c t", c=DC) for i in range(2)]
            assert TMPB + 3 * WB + 2 * 2048 <= TMP_W
            pt_rot = Rot("ptmp", [big_f32(extra0 + 1024 + i * 512, 512) for i in range(3)])
            wcnt = [0]
            for tt in range(NT):
                Ub = U_rot[tt % 2]
                ukey = lambda c: ("pu", tt % 2, c)
                for c in range(DC):
                    g = c // 2
                    win = 2 << g
                    base = tt * TT
                    src = HN32[:, c, base: base + WB]
                    rdk = [hn32_key(c, tt), ("hn32h", c)] + ([hn32_key(c, tt - 1)] if tt > 0 else [])
                    cur = src; curk = rdk; sh = 1
                    while sh < win:
                        wkk, wk = wk_rot.next()
                        lo = 2 * sh
                        wcnt[0] += 1
                        S.op(POOL if wcnt[0] % 3 == 2 else DVE,
                             (lambda e, wk=wk, cur=cur, sh=sh, lo=lo: e.tensor_tensor(out=wk[:, lo:WB], in0=cur[:, lo:WB], in1=cur[:, lo - sh:WB - sh], op=ALU.add)),
                             reads=curk, writes=[wkk])
                        cur = wk; curk = [wkk]; sh *= 2
                    S.op(DVE, (lambda e, cur=cur, c=c, win=win, src=src, Ub=Ub: e.scalar_tensor_tensor(
                        out=Ub[:, c, :], in0=cur[:, 16:WB], scalar=1.0 / win, in1=src[:, 16:WB], op0=ALU.mult, op1=ALU.subtract)),
                         reads=curk + rdk, writes=[ukey(c)])
                    if tt == 0:
                        wkk2, wk2 = wk_rot.next()
                        S.op(DVE, (lambda e, cur=cur, wk2=wk2, g=g: e.tensor_tensor(out=wk2[:, 0:16], in0=cur[:, 16:32],
                                                                                     in1=vec[:, VOFF["invcnt"] + g * 16: VOFF["invcnt"] + (g + 1) * 16], op=ALU.mult)),
                             reads=curk + ["vec"], writes=[wkk2])
                        S.op(DVE, (lambda e, wk2=wk2, c=c, src=src, Ub=Ub: e.tensor_tensor(out=Ub[:, c, 0:16], in0=wk2[:, 0:16], in1=src[:, 16:32], op=ALU.subtract)),
                             reads=[wkk2] + rdk, writes=[ukey(c)])
                for g in range(4):
                    for jo in range(2):
                        j = 2 * g + jo
                        pk_, pb_ = psbank("ffd", [4, 5])
                        for k in range(2):
                            S.op(PE, (lambda e, pb_=pb_, g=g, k=k, jo=jo, Ub=Ub: e.matmul(
                                pb_[:], lhsT=pw[:, g, k, jo * 128:(jo + 1) * 128], rhs=Ub[:, 2 * g + k, :], start=(k == 0), stop=(k == 1))),
                                 reads=[kw, ukey(2 * g + k)], writes=[pk_])
                        tk, tp = pt_rot.next()
                        S.op(ACT, (lambda e, tp=tp, pb_=pb_, j=j: e.activation(out=tp, in_=pb_[:], func=AF.Identity,
                                                                               scale=V("pool_scale", l2 * DC + j), bias=PBS(l2, j))),
                             reads=[pk_, "vec", "sm_pbs"], writes=[tk])
                        S.op(POOL, (lambda e, tp=tp, j=j, tt=tt: e.tensor_tensor(out=Hv[:, j, tsl(tt)], in0=Hv[:, j, tsl(tt)], in1=tp, op=ALU.add)),
                             reads=[tk, hkeys(j, tt)], writes=[hkeys(j, tt)])
            allk = [hn32_key(c, tt) for c in range(DC) for tt in range(NT)] + [("hn32h", c) for c in range(DC)] + ["pexs", "pexg", "phalo"]
            return allk

        def lru_layer(l):
            l2 = l // 2
            norm("mix_norm", l, hn_dst, hn_key)
            winx = [BIGreg[:, hf * 2560:(hf + 1) * 2560].bitcast(BF16).rearrange("p (k f) -> p k f", k=8) for hf in range(2)]
            GT = BIGreg[:, 5120:5120 + 1664].bitcast(BF16).rearrange("p (n f) -> p n f", n=2 * NGT)
            gts = [GT, GT]
            winv = win_d[l2].rearrange("(k p) f -> p k f", p=128)
            for hf in range(2):
                S.dma(POOL, (lambda e, hf=hf: e.dma_start(out=winx[hf], in_=winv[:, :, DR + hf * 640: DR + (hf + 1) * 640])), (asem[0], psem)[hf],
                      writes=[("A", hf)])

            def load_gt(hf):
                S.dma(POOL, (lambda e: e.dma_start(out=GT, in_=gt_d[l2, hf])), asem[1], writes=["gt"])
            b0 = 5120 + 1664
            XCb = [big_f32(b0 + i * 1280, 1280).bitcast(BF16).rearrange("p (c t) -> p c t", c=5) for i in range(2)]
            b0 += 2560 - 1280
            Yb = big_f32(b0 + 1280, 1280).bitcast(BF16).rearrange("p (c t) -> p c t", c=5)
            smallb = b0 + 2560
            HALO2 = [big_f32(smallb + i * 32, 30).rearrange("p (c t) -> p c t", c=RC) for i in range(2)]
            HALO0f = big_f32(smallb + 64, 30)
            STATE = big_f32(smallb + 96, RC)
            TSUM = big_f32(smallb + 112, RC * NT).rearrange("p (c t) -> p c t", c=RC)
            HC = [big_f32(smallb + 160 + i * 4, 3) for i in range(10)]
            HCT = [big_f32(smallb + 200 + i * 4, 3) for i in range(10)]
            HINIT = big_f32(smallb + 448, RC)
            TSM = big_f32(smallb + 464, RC)
            RS1 = big_f32(smallb + 480, RC)
            ykeys = [("y", ci) for ci in range(5)]
            EXS_ = big_f32(b0 + 1280, 128)
            EXG_ = big_f32(b0 + 1280 + 128, 512).rearrange("p (r f) -> p r f", r=4)
            EX1 = EXS_; EX1G = EXG_; EX2 = EXS_; EX2G = EXG_
            S.op(POOL, lambda e: e.memset(EXS_, 0.0), writes=["ex1", "ex2a", "ex2b"] + ykeys)
            XC2 = [tmp_f32(i * 2560, 2560).rearrange("p (c t) -> p c t", c=5) for i in range(2)]
            sk_, sslot = ring_reserve()
            tiles = [tmp_f32(5120 + i * 512, 512) for i in range((TMP_W - 5120) // 512)] + \
                    [big_f32(b0 + 3072 + i * 512, 512) for i in range((BIG_W - b0 - 3072) // 512)] + \
                    [sslot[:, i * 512:(i + 1) * 512] for i in range(SLOT_W // 512)]
            assert len(tiles) >= 10, len(tiles)
            extra = [ring_reserve(), ring_reserve()]
            tiles_p1 = tiles + [sl_[:, i * 512:(i + 1) * 512] for _, sl_ in extra for i in range(SLOT_W // 512)]
            tl_box = [Rot("tl", tiles_p1)]
            tlkeys = [("tl", i) for i in range(len(tiles))]
            xkeys = [k_ for k_, _ in extra] + [("tl", i) for i in range(len(tiles), len(tiles_p1))]

            pk3, pb3 = psbank("ffd", [4, 5])
            for c in range(RC):
                hf, ci = divmod(c, 5)
                for k in range(DC):
                    S.op(PE, (lambda e, c=c, hf=hf, ci=ci, k=k: e.matmul(pb3[:, c * 4: c * 4 + 3], lhsT=winx[hf][:, k, ci * 128:(ci + 1) * 128],
                                                                      rhs=HNb[:, k, T - 3:T], start=(k == 0), stop=(k == DC - 1))),
                         reads=[("A", hf), hn_key(k, 3)], writes=[pk3])
            S.op(ACT, lambda e: e.activation(out=EX1[:, 0:30].rearrange("p (c t) -> p c t", c=RC), in_=pb3[:, 0:40].rearrange("p (c t) -> p c t", c=RC)[:, :, 0:3], func=AF.Identity),
                 reads=[pk3], writes=["ex1"])
            exchange(EX1, 128, ["ex1"], EX1G, "ex1g")
            select_prev(HALO0f, EX1G[:, :, 0:30], 30, ["halo0"], "ex1g")
            S.op(DVE, lambda e: e.memset(sm[:, 502:503], 0.0),
                 writes=[("sq", 0), ("sq", 1), ("rs", 0), ("rs", 1), sk_] + [("xc", b_, ci) for b_ in range(2) for ci in range(5)] + tlkeys + xkeys)

            cw = lambda kk, c: V("lru_conv_w", l2 * 4 * RC + kk * RC + c)
            hc_i = [0]

            def stage1(u, hf, tt):
                wk = ("A", hf)
                XC = XC2[u % 2]; XCbb = XCb[u % 2]
                Hin = HALO2[tt % 2]; Hout = HALO2[(tt + 1) % 2]
                for ci in range(5):
                    c = hf * 5 + ci
                    xk, xps = psbank("xg3", [0, 1, 5]) if cur_pass[0] == 2 else psbank("xps5", [0, 1, 5, 6, 7])
                    for k in range(DC):
                        S.op(PE, (lambda e, xps=xps, k=k, ci=ci: e.matmul(xps[:], lhsT=winx[hf][:, k, ci * 128:(ci + 1) * 128], rhs=HNb[:, k, tsl(tt)],
                                                                        start=(k == 0), stop=(k == DC - 1))),
                             reads=[wk, hn_key(k, tt)], writes=[xk])
                    hi = hc_i[0] % 10; hc_i[0] += 1
                    hc, hct = HC[hi], HCT[hi]
                    hk = ("hc", hi)
                    S.op(POOL, (lambda e, hc=hc, c=c: e.tensor_scalar(out=hc[:, 0:3], in0=Hin[:, c, 0:3], scalar1=cw(0, c), scalar2=None, op0=ALU.mult)),
                         reads=[("halo", tt % 2, c), "vec"], writes=[hk])
                    S.op(POOL, (lambda e, hct=hct, c=c: e.tensor_scalar(out=hct[:, 0:2], in0=Hin[:, c, 1:3], scalar1=cw(1, c), scalar2=None, op0=ALU.mult)),
                         reads=[("halo", tt % 2, c), "vec"], writes=[("hct", hi)])
                    S.op(POOL, (lambda e, hc=hc, hct=hct: e.tensor_tensor(out=hc[:, 0:2], in0=hc[:, 0:2], in1=hct[:, 0:2], op=ALU.add)),
                         reads=[hk, ("hct", hi)], writes=[hk])
                    S.op(POOL, (lambda e, hct=hct, c=c: e.tensor_scalar(out=hct[:, 0:1], in0=Hin[:, c, 2:3], scalar1=cw(2, c), scalar2=None, op0=ALU.mult)),
                         reads=[("halo", tt % 2, c), "vec", hk], writes=[("hct", hi)])
                    S.op(POOL, (lambda e, hc=hc, hct=hct: e.tensor_tensor(out=hc[:, 0:1], in0=hc[:, 0:1], in1=hct[:, 0:1], op=ALU.add)),
                         reads=[hk, ("hct", hi)], writes=[hk])
                    S.op(ACT, (lambda e, xps=xps, ci=ci, c=c, XC=XC: e.activation(out=XC[:, ci, :], in_=xps[:], func=AF.Identity, scale=cw(3, c),
                                                                                bias=V("lru_conv_b", l2 * RC + c))),
                         reads=[xk, "vec"], writes=[("xc", u % 2, ci)])
                    S.op(ACT, (lambda e, xps=xps, c=c, Hout=Hout: e.activation(out=Hout[:, c, :], in_=xps[:, TT - 3:TT], func=AF.Identity)),
                         reads=[xk], writes=[("halo", (tt + 1) % 2, c)])
                    for kk in (2, 1, 0):
                        sh = 3 - kk
                        S.op(DVE, (lambda e, xps=xps, ci=ci, c=c, kk=kk, sh=sh, XC=XC: e.scalar_tensor_tensor(
                            out=XC[:, ci, sh:TT], in0=xps[:, 0:TT - sh], scalar=cw(kk, c), in1=XC[:, ci, sh:TT], op0=ALU.mult, op1=ALU.add)),
                             reads=[xk, "vec", ("xc", u % 2, ci)], writes=[("xc", u % 2, ci)])
                    S.op(DVE, (lambda e, ci=ci, hc=hc, XC=XC: e.tensor_tensor(out=XC[:, ci, 0:3], in0=XC[:, ci, 0:3], in1=hc[:, 0:3], op=ALU.add)),
                         reads=[hk, ("xc", u % 2, ci)], writes=[("xc", u % 2, ci)])
                    S.op(POOL, (lambda e, ci=ci, XC=XC, XCbb=XCbb: e.tensor_copy(XCbb[:, ci, :], XC[:, ci, :])), reads=[("xc", u % 2, ci)], writes=[("xcb", u % 2, ci)])
                    yield

            def stage2(u, pas, hf, tt, wing=None, wingk=None, wouts=None, woutk=None):
                wk = ("A", hf)
                XC = XC2[u % 2]; XCbb = XCb[u % 2]
                for grp in ([0, 1, 2], [3, 4]):
                    tl = {}
                    for cj in grp:
                        c = hf * 5 + cj
                        if pas == 2:
                            rk, r_ps = psbank("gate3", [2, 3, 4])
                            ik, i_ps = psbank("gate3", [2, 3, 4])
                        else:
                            rk, r_ps = psbank("gate3", [2, 3, 4])
                            ik, i_ps = psbank("gate3", [2, 3, 4])
                        nb = NBRS[cj]
                        for gi, (pk_, pb_) in enumerate(((rk, r_ps), (ik, i_ps))):
                            for n_, i in enumerate(nb):
                                S.op(PE, (lambda e, pb_=pb_, gi=gi, i=i, cj=cj, n_=n_, nb=nb, XCbb=XCbb: e.matmul(
                                    pb_[:], lhsT=gts[hf][:, gi * NGT + GT_IDX[(cj, i)], :], rhs=XCbb[:, i, :], start=(n_ == 0), stop=(n_ == len(nb) - 1))),
                                     reads=["gt", ("xcb", u % 2, i)], writes=[pk_])
                        Mk, M = tl_box[0].next()
                        Ik, I = tl_box[0].next()
                        Ak, A = tl_box[0].next()
                        tl[cj] = (Mk, M, Ik, I, Ak, A)
                        kw = dict(accum_out=TSUM[:, c, tt:tt + 1]) if pas == 1 else {}
                        S.op(ACT, (lambda e, M=M, r_ps=r_ps, c=c, kw=kw: e.activation(out=M, in_=r_ps[:], func=AF.Tanh, scale=0.5, bias=HBA(l2, c), **kw)),
                             reads=[rk, "sm_hb"], writes=[Mk] + ([("tsum", c)] if pas == 1 else []))
                        S.op(ACT, (lambda e, I=I, i_ps=i_ps, c=c: e.activation(out=I, in_=i_ps[:], func=AF.Tanh, scale=0.5, bias=HBX(l2, c))),
                             reads=[ik, "sm_hb"], writes=[Ik])
                        S.op(ACT, (lambda e, A=A, M=M, c=c: e.activation(out=A, in_=M, func=AF.Exp, scale=CH(l2, c), bias=CH(l2, c))),
                             reads=[Mk, "sm_c"], writes=[Ak])
                        S.op(ACT, (lambda e, M=M, c=c: e.activation(out=M, in_=M, func=AF.Exp, scale=CNEG(l2, c), bias=CNEG(l2, c))),
                             reads=[Mk, "sm_c"], writes=[Mk])
                        S.op(DVE, (lambda e, I=I, cj=cj, XC=XC: e.scalar_tensor_tensor(out=I, in0=I, scalar=1.0, in1=XC[:, cj, :], op0=ALU.add, op1=ALU.mult)),
                             reads=[Ik, ("xc", u % 2, cj)], writes=[Ik])
                        yield
                    for cj in grp:
                        Mk, M, Ik, I, Ak, A = tl[cj]
                        S.op(ACT, (lambda e, M=M: e.activation(out=M, in_=M, func=AF.Sqrt, scale=-1.0, bias=1.0)), reads=[Mk], writes=[Mk])
                    for cj in grp:
                        c = hf * 5 + cj
                        Mk, M, Ik, I, Ak, A = tl[cj]
                        S.op(DVE, (lambda e, I=I, M=M: e.scalar_tensor_tensor(out=I, in0=I, scalar=0.5, in1=M, op0=ALU.mult, op1=ALU.mult)),
                             reads=[Ik, Mk], writes=[Ik])
                        S.op(DVE, (lambda e, M=M, A=A, I=I, c=c: e.tensor_tensor_scan(out=M, data0=A, data1=I, initial=STATE[:, c:c + 1], op0=ALU.mult, op1=ALU.add)),
                             reads=[Ak, Ik, ("state", c), Mk], writes=[Mk])
                        S.op(POOL, (lambda e, M=M, c=c: e.tensor_copy(STATE[:, c:c + 1], M[:, TT - 1:TT])), reads=[Mk], writes=[("state", c)])
                    if pas == 2:
                        for cj in grp:
                            Mk, M, Ik, I, Ak, A = tl[cj]
                            gk_, g_ps = psbank("xg3", [0, 1, 5])
                            for k in range(DC):
                                S.op(PE, (lambda e, g_ps=g_ps, k=k, cj=cj: e.matmul(g_ps[:], lhsT=wing[:, k, cj * 128:(cj + 1) * 128], rhs=HNb[:, k, tsl(tt)],
                                                                                  start=(k == 0), stop=(k == DC - 1))),
                                     reads=[wingk, hn_key(k, tt)], writes=[gk_])
                            S.op(ACT, (lambda e, A=A, g_ps=g_ps: e.activation(out=A, in_=g_ps[:], func=AF.Gelu_apprx_tanh)), reads=[gk_, Ak], writes=[Ak])
                        for cj in grp:
                            Mk, M, Ik, I, Ak, A = tl[cj]
                            S.op(POOL, (lambda e, A=A, M=M, cj=cj: e.tensor_tensor(out=Yb[:, cj, :], in0=M, in1=A, op=ALU.mult)), reads=[Ak, Mk], writes=[("y", cj)])
                if pas == 2:
                    for j in range(DC):
                        ok_, o_ps = psbank("norm", [6, 7])
                        for ci in range(5):
                            S.op(PE, (lambda e, o_ps=o_ps, ci=ci, j=j: e.matmul(o_ps[:], lhsT=wouts[:, ci, j * 128:(j + 1) * 128], rhs=Yb[:, ci, :],
                                                                              start=(ci == 0), stop=(ci == 4))),
                                 reads=[woutk, ("y", ci)], writes=[ok_])
                        S.op(DVE, (lambda e, o_ps=o_ps, j=j: e.tensor_tensor(out=Hv[:, j, tsl(tt)], in0=o_ps[:], in1=Hv[:, j, tsl(tt)], op=ALU.add)),
                             reads=[ok_, hkeys(j, tt)], writes=[hkeys(j, tt)])

            def reset_run(init_state_ap):
                S.op(POOL, lambda e: e.tensor_copy(HALO2[0].rearrange("p c t -> p (c t)") if False else big_f32(smallb, 30), HALO0f), reads=["halo0"],
                     writes=[("halo", 0, c) for c in range(RC)])
                if init_state_ap is None:
                    S.op(POOL, lambda e: e.memset(STATE, 0.0), writes=[("state", c) for c in range(RC)])
                else:
                    S.op(POOL, lambda e: e.tensor_copy(STATE, init_state_ap), reads=["hinit"], writes=[("state", c) for c in range(RC)])

            cur_pass = [1]

            def run_pass(pas, wts):
                cur_pass[0] = pas
                units = [(hf, tt) for hf in range(2) for tt in range(NT)]
                for _ in stage1(0, *units[0]):
                    pass
                for u, (hf, tt) in enumerate(units):
                    if tt == 0:
                        load_gt(hf)
                        w_ = wts(hf) if pas == 2 else (None,) * 4
                    g1 = stage1(u + 1, *units[u + 1]) if u + 1 < len(units) else iter(())
                    if not INTERLEAVE:
                        for _ in g1:
                            pass
                    for _ in stage2(u, pas, hf, tt, *w_):
                        next(g1, None)
                    for _ in g1:
                        pass

            reset_run(None)
            run_pass(1, None)
            allts = [("tsum", c) for c in range(RC)]
            S.op(DVE, lambda e: e.tensor_reduce(out=RS1, in_=TSUM, axis=mybir.AxisListType.X, op=ALU.add), reads=allts, writes=["rs1"])
            S.op(DVE, lambda e: e.scalar_tensor_tensor(out=RS1, in0=RS1, scalar=float(T), in1=sm[:, 20 + l2 * RC: 20 + (l2 + 1) * RC], op0=ALU.add, op1=ALU.mult),
                 reads=["rs1", "sm_c"], writes=["rs1"])
            S.op(ACT, lambda e: e.activation(out=EX2[:, 0:RC], in_=RS1, func=AF.Exp), reads=["rs1", "ex1g", "halo0"], writes=["ex2a", "ex1"])
            S.op(POOL, lambda e: e.tensor_copy(EX2[:, RC:2 * RC], STATE), reads=[("state", c) for c in range(RC)] + ["ex1g", "halo0"], writes=["ex2b", "ex1"])
            exchange(EX2, 128, ["ex2a", "ex2b"], EX2G, "ex1g")
            S.op(DVE, lambda e: e.memset(HINIT, 0.0), writes=["hinit"])
            for j in range(3):
                S.op(DVE, (lambda e, j=j: e.tensor_tensor(out=TSM, in0=EX2G[:, j, 0:RC], in1=HINIT, op=ALU.mult)), reads=["ex1g", "hinit"], writes=["tsm"])
                S.op(DVE, (lambda e, j=j: e.tensor_tensor(out=TSM, in0=TSM, in1=EX2G[:, j, RC:2 * RC], op=ALU.add)), reads=["ex1g", "tsm"], writes=["tsm"])
                S.op(DVE, (lambda e, j=j: e.tensor_tensor(out=TSM, in0=TSM, in1=HINIT, op=ALU.subtract)), reads=["hinit", "tsm"], writes=["tsm"])
                S.op(DVE, (lambda e, j=j: e.scalar_tensor_tensor(out=HINIT, in0=TSM, scalar=V("msk", j), in1=HINIT, op0=ALU.mult, op1=ALU.add)),
                     reads=["tsm", "hinit", "vec"], writes=["hinit"])
            woutv = wout_d[l2].rearrange("(k p) d -> p k d", p=128)
            reset_run(HINIT)
            S.op(DVE, lambda e: e.memset(sm[:, 504:505], 0.0), writes=xkeys)
            ring_release(); ring_release()
            tl_box[0] = Rot("tl", tiles)

            def load_w2(hf):
                wingk, sl1 = ring_piece([(lambda s: s[:, 0:5120].rearrange("p (k f) -> p k f", k=8), winv[:, :, hf * 640:(hf + 1) * 640])])
                wing = sl1[:, 0:5120].rearrange("p (k f) -> p k f", k=8)
                woutk, sl2 = ring_piece([(lambda s: s[:, 0:5120].rearrange("p (k d) -> p k d", k=5), woutv[:, hf * 5:(hf + 1) * 5, :])])
                wouts = sl2[:, 0:5120].rearrange("p (k d) -> p k d", k=5)
                return (wing, wingk, wouts, woutk)
            run_pass(2, load_w2)
            S.op(DVE, lambda e: e.memset(sm[:, 503:504], 0.0), writes=[sk_] + tlkeys)
            ring_release()

        for l in layers:
            if "1" in phases:
                fence()
                ffn(l, f1g, f1u, f1d, "ffn1_norm")
            if "m" in phases:
                fence()
                if l % 2 == 0:
                    lru_layer(l)
                else:
                    pool_layer(l)
            if "2" in phases:
                fence()
                ffn(l, f2g, f2u, f2d, "ffn2_norm", mid_hook=(lambda l=l: ple_prefetch(l)) if "p" in phases else None)
            if "p" in phases:
                fence()
                if l not in ple_bufs:
                    ple_prefetch(l)
                ple(l)
        fence()

        ov = out_d.rearrange("(c p) t -> p c t", p=128)
        OUTS = [BIGreg[:, i * 4096:(i + 1) * 4096].rearrange("p (c t) -> p c t", c=DC) for i in range(2)]
        for tt in range(NT):
            ob = OUTS[tt % 2]
            if final:
                norm("final_norm", 0, lambda c, tt_, ob=ob: ob[:, c, :], lambda c, tt_: ("outs", tt_ % 2), tts=(tt,))
            else:
                for c in range(DC):
                    S.op(POOL, (lambda e, ob=ob, c=c, tt=tt: e.tensor_copy(ob[:, c, :], Hv[:, c, tsl(tt)])), reads=[hkeys(c, tt)], writes=[("outs", tt % 2)])
            S.dma(SP, (lambda e, ob=ob, tt=tt: e.dma_start(out=ov[:, :, tsl(tt)], in_=ob)), osem[tt % 2], reads=[("outs", tt % 2)],
                  writes=[("outd", tt), ("outs", tt % 2)])
        S.barrier(SP, [("outd", tt) for tt in range(NT)])
        S.emit(block, esem)
        build.nops = len(S.ops)
    return nc


VEC_SPECS.append(("eps", 1))
VOFF["eps"] = NV
NV += 1


def _prep_inputs(inp):
    x = np.asarray(inp["x"], np.float32)
    p = np.asarray(inp["p"], np.float32)
    gt = np.zeros((2, 2, 128, 2 * NGT, 128), np.float32)
    for l2 in range(2):
        for gi, nm in enumerate(("lru_w_a", "lru_w_x")):
            w = np.asarray(inp[nm], np.float32)[l2]
            dense = np.zeros((DR, DR), np.float32)
            for h in range(16):
                dense[h * 80:(h + 1) * 80, h * 80:(h + 1) * 80] = w[h]
            for hf in range(2):
                for (j, i), n in GT_IDX.items():
                    gt[l2, hf, :, gi * NGT + n, :] = dense[(hf * 5 + i) * 128:(hf * 5 + i + 1) * 128, (hf * 5 + j) * 128:(hf * 5 + j + 1) * 128]
    common_vec = {}
    for nm in ("ffn1_norm", "mix_norm", "ffn2_norm", "ple_norm", "final_norm", "pool_b", "pool_scale",
               "lru_conv_w", "lru_conv_b", "lru_b_a", "lru_b_x", "lru_a_param"):
        common_vec[nm] = _colpack(np.asarray(inp[nm], np.float32))
    shared = {k: np.ascontiguousarray(np.asarray(inp[k], np.float32)) for k in
              ("ffn1_w_gate", "ffn1_w_up", "ffn1_w_down", "ffn2_w_gate", "ffn2_w_up", "ffn2_w_down",
               "lru_w_in", "lru_w_out", "pool_w", "ple_w_gate", "ple_w_proj")}
    shared["lru_gt"] = gt
    in_maps = []
    for c in range(NCORE):
        b, k = divmod(c, 4)
        vecs = np.zeros((128, NV), np.float32)
        for nm, arr in common_vec.items():
            vecs[:, VOFF[nm]:VOFF[nm] + arr.shape[1]] = arr
        sel = np.zeros(4, np.float32)
        if k > 0:
            sel[k - 1] = 1.0
        msk = np.zeros(4, np.float32)
        msk[:k] = 1.0
        vecs[:, VOFF["sel"]:VOFF["sel"] + 4] = sel[None, :]
        vecs[:, VOFF["msk"]:VOFF["msk"] + 4] = msk[None, :]
        ic = np.zeros((4, 16), np.float32)
        for g in range(4):
            win = 2 << g
            for t in range(16):
                ic[g, t] = np.float32(1.0) / np.float32(min(t + 1, win) if k == 0 else win)
        vecs[:, VOFF["invcnt"]:VOFF["invcnt"] + 64] = ic.reshape(1, 64)
        vecs[:, VOFF["eps"]] = EPS
        m = dict(shared)
        m["xT"] = np.ascontiguousarray(x[b, k * T:(k + 1) * T, :].T)
        m["pT"] = np.ascontiguousarray(p[:, b, k * T:(k + 1) * T, :].transpose(0, 2, 1))
        m["vecs"] = vecs
        in_maps.append(m)
    return in_maps


_NC_CACHE = {}


def kernel(**inputs):
    in_maps = _prep_inputs(inputs)
    if "nc" not in _NC_CACHE:
        _NC_CACHE["nc"] = build()
    nc = _NC_CACHE["nc"]
    res = run_bass_kernel_spmd(nc, in_maps, core_ids=list(range(NCORE)))
    out = np.empty((2, 4 * T, D), np.float32)
    for c in range(NCORE):
        b, k = divmod(c, 4)
        out[b, k * T:(k + 1) * T, :] = res.results[c]["outT"].T
    return out
```
